# Optimizing a Trainium2 kernel written in Bass

```python
import math
import jax, jax.numpy as jnp
from jax import lax
import numpy as np

D_MODEL = 1024
BATCH = 4
SEQ = 8192
DEPTH = 2

N_EVEN = (DEPTH + 1) // 2
N_ODD = DEPTH // 2

HG_HEADS = 4
HG_DK = 128
HG_DV = 128
HG_WIDTH = HG_HEADS * HG_DK
HG_CHUNK = 32

CV_CH = D_MODEL - HG_HEADS * HG_DV
CV_K = 31

IN_COLS = 4 * HG_WIDTH + 2 * CV_CH

POOL_WINDOWS = (2, 4, 8, 16)
POOL_GROUPS = len(POOL_WINDOWS)
POOL_CH = D_MODEL // POOL_GROUPS

MEM_LEN = 256
XA_HEADS = 4
XA_DH = D_MODEL // XA_HEADS

D_FF = -(-(8 * D_MODEL) // (3 * 256)) * 256

ALPHA = (2 * DEPTH) ** 0.25
BETA = (8 * DEPTH) ** -0.25
LN_EPS = 1e-5
RMS_EPS = 1e-6

kernel_name = "hgrn2_conformer_pool_hybrid_deepnorm"


def layer_norm(x, g, b):
    xf = x.astype(jnp.float32)
    mu = jnp.mean(xf, axis=-1, keepdims=True)
    var = jnp.mean(jnp.square(xf - mu), axis=-1, keepdims=True)
    y = (xf - mu) * lax.rsqrt(var + LN_EPS) * g.astype(jnp.float32) + b.astype(jnp.float32)
    return y.astype(x.dtype)


def hgrn2_chunked(q, k, v, log_f):
    B, S, H, DK = q.shape
    DV = v.shape[-1]
    nc = S // HG_CHUNK

    def to_chunks(t):
        return t.astype(jnp.float32).reshape(B, nc, HG_CHUNK, H, t.shape[-1]).transpose(0, 3, 1, 2, 4)

    q, k, v, log_f = to_chunks(q), to_chunks(k), to_chunks(v), to_chunks(log_f)
    b = jnp.cumsum(log_f, axis=3)
    b_last = b[:, :, :, -1:, :]
    q_t = q * jnp.exp(b)
    k_t = k * jnp.exp(-b)
    causal = jnp.tril(jnp.ones((HG_CHUNK, HG_CHUNK), dtype=bool))
    scores = jnp.einsum('bhntd,bhnsd->bhnts', q_t, k_t)
    scores = jnp.where(causal, scores, 0.0)
    o_intra = jnp.einsum('bhnts,bhnsv->bhntv', scores, v)
    k_end = k * jnp.exp(b_last - b)
    d_state = jnp.einsum('bhnsd,bhnsv->bhndv', k_end, v)
    decay = jnp.exp(b_last[:, :, :, 0, :])

    def step(state, inp):
        dec, ds = inp
        return dec[..., None] * state + ds, state

    s0 = jnp.zeros((B, H, DK, DV), jnp.float32)
    _, s_start = lax.scan(step, s0, (jnp.moveaxis(decay, 2, 0), jnp.moveaxis(d_state, 2, 0)))
    s_start = jnp.moveaxis(s_start, 0, 2)
    o_inter = jnp.einsum('bhntd,bhndv->bhntv', q_t, s_start)
    o = o_intra + o_inter
    return o.transpose(0, 2, 3, 1, 4).reshape(B, S, H, DV)


def hybrid_ab_mixer(x, w_in, lb, hg_norm_g, cv_w, cv_b, cv_ln_g, cv_ln_b, w_out):
    B, S, _ = x.shape
    h = x @ w_in
    q, f_pre, i_v, g, a, a_gate = jnp.split(
        h, [HG_WIDTH, 2 * HG_WIDTH, 3 * HG_WIDTH, 4 * HG_WIDTH, 4 * HG_WIDTH + CV_CH], axis=-1)

    lb = lb.astype(jnp.float32)
    f_pre = f_pre.astype(jnp.float32)
    f = lb + (1.0 - lb) * jax.nn.sigmoid(f_pre)
    k = (1.0 - lb) * jax.nn.sigmoid(-f_pre)
    log_f = jnp.log(f)
    heads = lambda t: t.reshape(B, S, HG_HEADS, -1)
    o = hgrn2_chunked(heads(q), heads(k), heads(i_v), heads(log_f))
    o = o * lax.rsqrt(jnp.mean(jnp.square(o), axis=-1, keepdims=True) + RMS_EPS)
    o = o.reshape(B, S, HG_HEADS * HG_DV) * hg_norm_g.astype(jnp.float32) * jax.nn.silu(g.astype(jnp.float32))

    u = a * jax.nn.sigmoid(a_gate)
    u = lax.conv_general_dilated(
        u, cv_w.astype(u.dtype), window_strides=(1,), padding=[(CV_K - 1, 0)],
        dimension_numbers=('NWC', 'WIO', 'NWC'), feature_group_count=CV_CH) + cv_b
    u = jax.nn.silu(layer_norm(u, cv_ln_g, cv_ln_b))

    y = jnp.concatenate([o.astype(x.dtype), u.astype(x.dtype)], axis=-1)
    return y @ w_out


def multiscale_pool_mixer(x, pool_w, pool_scale):
    B, S, _ = x.shape
    xg = x.astype(jnp.float32).reshape(B, S, POOL_GROUPS, POOL_CH)
    cs = jnp.cumsum(xg, axis=1)
    pos = jnp.arange(1, S + 1, dtype=jnp.float32)
    feats = []
    for gi, w in enumerate(POOL_WINDOWS):
        c = cs[:, :, gi]
        lagged = jnp.pad(c, ((0, 0), (w, 0), (0, 0)))[:, :S]
        mean = (c - lagged) / jnp.minimum(pos, float(w))[:, None]
        feats.append(mean - xg[:, :, gi])
    p = jnp.stack(feats, axis=2).astype(x.dtype)
    y = jnp.einsum('bsgc,gcd->bsgd', p, pool_w).reshape(B, S, D_MODEL)
    return y * pool_scale


def memory_cross_attention(x, mem_n, w_q, w_k, w_v, w_o):
    B, S, _ = x.shape
    q = (x @ w_q).reshape(B, S, XA_HEADS, XA_DH)
    k = (mem_n @ w_k).reshape(B, -1, XA_HEADS, XA_DH)
    v = (mem_n @ w_v).reshape(B, -1, XA_HEADS, XA_DH)
    s = jnp.einsum('bshd,bmhd->bhsm', q, k).astype(jnp.float32) * (XA_DH ** -0.5)
    p = jax.nn.softmax(s, axis=-1).astype(x.dtype)
    o = jnp.einsum('bhsm,bmhd->bshd', p, v).reshape(B, S, D_MODEL)
    return o @ w_o


def swiglu_ffn(x, w_gate, w_up, w_down):
    return (jax.nn.silu(x @ w_gate) * (x @ w_up)) @ w_down


def setup_inputs(seed: int = 0) -> dict:
    key = jax.random.key(seed)
    ks = jax.random.split(key, 32)
    f32 = jnp.float32
    nrm = lambda k, shape, scale: jax.random.normal(k, shape, f32) * scale
    gain = lambda k, shape: 1.0 + 0.02 * jax.random.normal(k, shape, f32)
    bias = lambda k, shape: 0.02 * jax.random.normal(k, shape, f32)
    d = D_MODEL
    return {
        "x": jax.random.normal(ks[0], (BATCH, SEQ, d), f32),
        "mem": jax.random.normal(ks[1], (BATCH, MEM_LEN, d), f32),
        "lb_param": nrm(ks[2], (DEPTH + 1, HG_WIDTH), 0.1),
        "mem_ln_g": gain(ks[3], (d,)),
        "mem_ln_b": bias(ks[4], (d,)),
        "ab_w_in": nrm(ks[5], (N_EVEN, d, IN_COLS), d ** -0.5),
        "hg_norm_g": gain(ks[6], (N_EVEN, HG_HEADS * HG_DV)),
        "cv_w": nrm(ks[7], (N_EVEN, CV_K, 1, CV_CH), CV_K ** -0.5),
        "cv_b": bias(ks[8], (N_EVEN, CV_CH)),
        "cv_ln_g": gain(ks[9], (N_EVEN, CV_CH)),
        "cv_ln_b": bias(ks[10], (N_EVEN, CV_CH)),
        "ab_w_out": nrm(ks[11], (N_EVEN, d, d), BETA * d ** -0.5),
        "pool_w": nrm(ks[12], (N_ODD, POOL_GROUPS, POOL_CH, POOL_CH), BETA * POOL_CH ** -0.5),
        "pool_scale": gain(ks[13], (N_ODD, d)),
        "ln_mix_g": gain(ks[14], (DEPTH, d)),
        "ln_mix_b": bias(ks[15], (DEPTH, d)),
        "xa_wq": nrm(ks[16], (DEPTH, d, d), d ** -0.5),
        "xa_wk": nrm(ks[17], (DEPTH, d, d), d ** -0.5),
        "xa_wv": nrm(ks[18], (DEPTH, d, d), d ** -0.5),
        "xa_wo": nrm(ks[19], (DEPTH, d, d), BETA * d ** -0.5),
        "ln_xa_g": gain(ks[20], (DEPTH, d)),
        "ln_xa_b": bias(ks[21], (DEPTH, d)),
        "ffn_wg": nrm(ks[22], (DEPTH, d, D_FF), d ** -0.5),
        "ffn_wu": nrm(ks[23], (DEPTH, d, D_FF), d ** -0.5),
        "ffn_wd": nrm(ks[24], (DEPTH, D_FF, d), BETA * D_FF ** -0.5),
        "ln_ffn_g": gain(ks[25], (DEPTH, d)),
        "ln_ffn_b": bias(ks[26], (DEPTH, d)),
    }


def reference(x, mem, lb_param, mem_ln_g, mem_ln_b, ab_w_in, hg_norm_g, cv_w, cv_b,
              cv_ln_g, cv_ln_b, ab_w_out, pool_w, pool_scale, ln_mix_g, ln_mix_b,
              xa_wq, xa_wk, xa_wv, xa_wo, ln_xa_g, ln_xa_b, ffn_wg, ffn_wu, ffn_wd,
              ln_ffn_g, ln_ffn_b):
    lb_all = jnp.cumsum(jax.nn.softmax(lb_param.astype(jnp.float32), axis=0), axis=0)
    mem_n = layer_norm(mem, mem_ln_g, mem_ln_b)
    for l in range(DEPTH):
        if l % 2 == 0:
            e = l // 2
            y = hybrid_ab_mixer(x, ab_w_in[e], lb_all[l], hg_norm_g[e], cv_w[e], cv_b[e],
                                cv_ln_g[e], cv_ln_b[e], ab_w_out[e])
        else:
            o = l // 2
            y = multiscale_pool_mixer(x, pool_w[o], pool_scale[o])
        x = layer_norm(ALPHA * x + y, ln_mix_g[l], ln_mix_b[l])
        y = memory_cross_attention(x, mem_n, xa_wq[l], xa_wk[l], xa_wv[l], xa_wo[l])
        x = layer_norm(ALPHA * x + y, ln_xa_g[l], ln_xa_b[l])
        y = swiglu_ffn(x, ffn_wg[l], ffn_wu[l], ffn_wd[l])
        x = layer_norm(ALPHA * x + y, ln_ffn_g[l], ln_ffn_b[l])
    return x
```

```python
import os
import sys
import numpy as np
from contextlib import ExitStack
import concourse.bass as bass
import concourse.mybir as mybir
from concourse.bass_utils import run_bass_kernel_spmd

F32 = mybir.dt.float32
BF16 = mybir.dt.bfloat16
AF = mybir.ActivationFunctionType
ALU = mybir.AluOpType

P = 128
D = 1024
KC = 8
T = 512
DFF = 2816
NJ = 22
MEM = 256
ALPHA = 4.0 ** 0.25
LN_EPS = 1e-5
RMS_EPS = 1e-6
CVK = 31
HALO = 30
SLOT = 2048
NSLOT = 7
NSTG = 1
NIO = 3
NTF = 11
NG_FULL = 8


class Op:
    __slots__ = ("eng", "fn", "dma", "npieces", "deps", "stream", "sidx", "signal", "waits", "done", "line")

    def __init__(self, eng, fn, dma, npieces):
        self.eng = eng
        self.fn = fn
        self.dma = dma
        self.npieces = npieces
        self.signal = False


class Prog:
    def __init__(self):
        self.ops = []
        self.lastw = {}
        self.readers = {}

    def add(self, eng, fn, reads=(), writes=(), dma=None, npieces=1):
        op = Op(eng, fn, dma, npieces)
        op.stream = dma if dma is not None else eng
        f_ = sys._getframe(1)
        if f_.f_code.co_name == "A":
            f_ = f_.f_back
        op.line = f_.f_lineno
        deps = set()
        psr = [k for k in reads if isinstance(k, tuple) and k[0] == "ps"]
        if psr:
            reads = [k for k in reads if k not in psr]
            writes = list(writes) + psr
        for k in reads:
            w = self.lastw.get(k)
            if w is not None:
                deps.add(w)
        for k in writes:
            w = self.lastw.get(k)
            if w is not None:
                deps.add(w)
            rd = self.readers.get(k)
            if rd:
                deps.update(rd.values())
        for k in reads:
            self.readers.setdefault(k, {})[op.stream] = op
        for k in writes:
            self.lastw[k] = op
            self.readers[k] = {}
        deps.discard(op)
        op.deps = deps
        self.ops.append(op)
        return op

    def finalize(self):
        streams = {}
        for op in self.ops:
            lst = streams.setdefault(op.stream, [])
            op.sidx = len(lst)
            lst.append(op)
        clocks = {}
        for op in self.ops:
            clk = clocks.setdefault(op.eng, {})
            need = {}
            for d in op.deps:
                if d.stream == op.eng and op.eng == "pe":
                    continue
                if clk.get(d.stream, -1) < d.sidx:
                    if need.get(d.stream, -1) < d.sidx:
                        need[d.stream] = d.sidx
            op.waits = sorted(need.items())
            for s_, i_ in op.waits:
                dop = streams[s_][i_]
                dop.signal = True
                for k2, v2 in dop.done.items():
                    if clk.get(k2, -1) < v2:
                        clk[k2] = v2
            done = dict(clk)
            done[op.stream] = op.sidx
            op.done = done
            if op.dma is None and op.eng == "pe":
                clk["pe"] = op.sidx
        if os.environ.get("MK_ALLSIG"):
            for op in self.ops:
                if op.dma is None and op.eng != "sp":
                    op.signal = True
        self.val = {}
        for s_, lst in streams.items():
            cnt = 0
            vals = []
            for op in lst:
                if op.dma is not None:
                    cnt += 16 * op.npieces
                elif op.signal:
                    cnt += 1
                vals.append(cnt)
            self.val[s_] = vals
        self.streams = streams
        for op in self.ops:
            op.done = None

    def emit(self, nc, es):
        sems = {}
        for s_ in self.streams:
            sems[s_] = es.enter_context(nc.semaphore("sem_" + s_))
            if os.environ.get("MK_DUMP"):
                print("MKSEM", s_, sems[s_].num)
        blk = es.enter_context(nc.Block())
        eng_ops = {}
        for op in self.ops:
            eng_ops.setdefault(op.eng, []).append(op)

        def mk(eng):
            def body(e):
                for op in eng_ops.get(eng, []):
                    for s_, i_ in op.waits:
                        e.wait_ge(sems[s_], self.val[s_][i_])
                    if op.dma is not None:
                        op.fn(e, sems[op.dma])
                    else:
                        ins = op.fn(e)
                        if op.signal:
                            ins.then_inc(sems[eng], 1)
            return body

        blk.sync(mk("sp"))
        blk.tensor(mk("pe"))
        blk.scalar(mk("act"))
        blk.vector(mk("dve"))
        blk.gpsimd(mk("pool"))


def _cf_layout():
    cols = {}
    off = 0
    for name, n in [("ident", 128), ("scanmask", 512), ("cmask", 128), ("invcnt", 64), ("flag", 1),
                    ("lbp", 12), ("memg", 8), ("memb", 8), ("hgn", 4), ("cvw", 124), ("cvb", 4),
                    ("cvg", 4), ("cvbeta", 4), ("pscale", 8),
                    ("lnmix_g", 16), ("lnmix_b", 16), ("lnxa_g", 16), ("lnxa_b", 16),
                    ("lnffn_g", 16), ("lnffn_b", 16)]:
        cols[name] = (off, n)
        off += n
    return cols, off


CF_COLS, NCF = _cf_layout()


def _vec_cols(v):
    v = np.asarray(v, np.float32)
    if v.ndim == 1:
        v = v[None]
    L = v.shape[0]
    n = v.shape[1] // 128
    return np.ascontiguousarray(v.reshape(L, n, 128).transpose(2, 0, 1).reshape(128, L * n))


def build(NG, enable=("mix0", "xa0", "ffn0", "mix1", "xa1", "ffn1"), use_prefix=True):
    nc = bass.Bass("TRN2", target_bir_lowering=False)
    NTOK = NG * T
    dt = lambda name, shape, kind="ExternalInput", dtype=F32: nc.dram_tensor(name, shape, dtype, kind=kind).ap()
    xcur = dt("xcur", [NTOK, D])
    xprev = dt("xprev", [NTOK, D])
    memd = dt("mem", [MEM, D])
    cfd = dt("cf", [P, NCF])
    WSHAPE = {"win": ("w_in", [D, 3072]), "wout": ("w_out", [D, D]), "pw": ("pool_w", [D, 256])}
    for l in range(2):
        for nm_, shp in [("wq", [D, D]), ("wk", [D, D]), ("wv", [D, D]), ("wo", [D, D]), ("wg", [D, DFF]), ("wu", [D, DFF]), ("wd", [DFF, D])]:
            WSHAPE[f"{nm_}{l}"] = (f"{nm_}{l}", shp)

    class _W(dict):
        def __missing__(self, k):
            nm_, shp = WSHAPE[k]
            self[k] = dt(nm_, shp)
            return self[k]
    W = _W()
    outd = dt("out", [NTOK, D], kind="ExternalOutput")
    NPAN = 200 if len(enable) > 0 else 1
    scr = dt("wscr", [NPAN, P, SLOT], kind="Internal", dtype=BF16)

    pg = Prog()
    es = ExitStack()
    E = es.enter_context
    sb = lambda name, shape, dtype=F32: E(nc.sbuf_tensor("sb_" + name, shape, dtype))

    io = sb("io", [P, NIO, D])
    cf = sb("cf", [P, NCF])
    dv = sb("dv", [P, 128])
    xres = sb("xres", [P, 2, KC, T])
    xb = sb("xb", [P, 2, KC, T], BF16)
    TFL = sb("TFL", [P, 2, 2, T])
    G = sb("G", [P, 24, T], BF16)
    TF = sb("TF", [P, NTF, T])
    kendT = sb("kendT", [P, 4, T], BF16)
    vT = sb("vT", [P, 4, T], BF16)
    ubuf = sb("ubuf", [P, 4, HALO + T], BF16)
    diag = sb("diag", [P, CVK, P], BF16)
    S = sb("S", [P, 4, P])
    Sb = sb("Sb", [P, 8, 4, P], BF16)
    dcy = sb("dcy", [P, 4, 16])
    KT = sb("KT", [P, 2, KC, MEM], BF16)
    Vm = sb("Vm", [P, 2, 2, D], BF16)
    slots = sb("slots", [P, NSLOT, SLOT], BF16)
    ptmp = sb("ptmp", [P, 2, 16 + T])
    P8 = sb("P8", [P, KC, T], BF16)
    phalo = sb("phalo", [P, KC, 16])
    identb = sb("identb", [P, P], BF16)
    cmaskb = sb("cmaskb", [P, P], BF16)
    onesb = sb("onesb", [P, 4, P], BF16)
    ps = [E(nc.psum_tensor(f"ps{i}", [P, T], F32)) for i in range(8)]

    def cfc(name, a=0, n=None):
        o, w = CF_COLS[name]
        if n is None:
            n = w - a
        return cf[:, o + a:o + a + n]

    ident = cfc("ident")

    rr = {"A": 0, "B": 0, "C": 0, "L": 0}
    pools = {"A": [0, 1, 2], "L": [3, 4], "C": [5, 6], "B": [7]}

    dq = {"on": False, "q": [], "stream": None}

    def pump(n):
        if dq["on"]:
            return
        for _ in range(n):
            if not dq["q"]:
                return
            eng, fn, reads, writes, dma = dq["q"].pop(0)
            pg.add(eng, fn, reads, writes, dma=dma)

    def flush():
        pump(10 ** 9)

    def pbank(pool):
        pump(2)
        lst = pools[pool]
        b = lst[rr[pool] % len(lst)]
        rr[pool] += 1
        return b

    def A(eng, fn, reads=(), writes=(), dma=None):
        if dq["on"]:
            dq["q"].append((eng, fn, list(reads), list(writes), dma))
            return None
        return pg.add(eng, fn, reads, writes, dma=dma)

    def KX(s, c):
        return ("xres", s, c)

    def KB(s, c):
        return ("xb", s, c)

    def KG(i):
        return ("G", i)

    def KT_(i):
        return ("TF", i)

    def KP(b):
        return ("ps", b)

    wstate = {"slot": 0, "stg": 0, "gid": {}, "scrw": 0}

    def get_panel(wname, kc0, nk, col0, ncols, to_scratch=True):
        n = nk * ncols
        assert n <= SLOT
        key = (wname, kc0, nk, col0, ncols)
        sl = wstate["slot"] % NSLOT
        wstate["slot"] += 1
        dst = slots[:, sl, 0:n]
        view = dst.rearrange("p (k n) -> p k n", k=nk)
        if key in wstate["gid"]:
            gid = wstate["gid"][key]
            pg.add("sp", lambda e, sem, dst=dst, gid=gid, n=n: e.dma_start(out=dst, in_=scr[gid, :, 0:n]).then_inc(sem, 16),
                   reads=[("scr", gid)], writes=[("w", sl)], dma=f"w{sl}")
        else:
            gid = len(wstate["gid"])
            assert gid < NPAN
            wstate["gid"][key] = gid
            src = W[wname][kc0 * P:(kc0 + nk) * P, col0:col0 + ncols].rearrange("(k p) n -> p k n", p=P)
            pg.add("pool", lambda e, sem, view=view, src=src: e.dma_start(out=view, in_=src).then_inc(sem, 16),
                   writes=[("w", sl)], dma=f"wl{sl}")
            if to_scratch:
                pg.add("sp", lambda e, sem, dst=dst, gid=gid, n=n: e.dma_start(out=scr[gid, :, 0:n], in_=dst).then_inc(sem, 16),
                       reads=[("w", sl)], writes=[("scr", gid)], dma=f"sw{sl}")
        return sl, view

    def proj_fm(wname, nk, col_chunks, act_ap, act_keys, evac, N=T, kc_base=0, cols_per_panel=256):
        cpp = cols_per_panel // P
        i = 0
        while i < len(col_chunks):
            grp = [col_chunks[i]]
            while len(grp) < cpp and i + len(grp) < len(col_chunks) and col_chunks[i + len(grp)] == grp[-1] + 1:
                grp.append(col_chunks[i + len(grp)])
            sl, view = get_panel(wname, kc_base, nk, grp[0] * P, len(grp) * P)
            for gi_, oc in enumerate(grp):
                b = pbank("A")

                def fn(e, b=b, view=view, gi_=gi_):
                    ins = None
                    for kc in range(nk):
                        ins = e.matmul(ps[b][:, 0:N], lhsT=view[:, kc, gi_ * P:(gi_ + 1) * P], rhs=act_ap(kc),
                                       start=(kc == 0), stop=(kc == nk - 1))
                    return ins
                A("pe", fn, reads=[("w", sl)] + list(act_keys), writes=[KP(b)])
                evac(i + gi_, oc, b)
            i += len(grp)

    pg.add("sp", lambda e, sem: e.dma_start(out=cf[:], in_=cfd).then_inc(sem, 16), writes=["cf"], dma="cfl")
    A("pool", lambda e: e.memset(onesb[:, 0, :], 1.0 / 1024), writes=["ones0"])
    A("pool", lambda e: e.memset(onesb[:, 1, :], 1.0 / 512), writes=["ones1"])
    A("pool", lambda e: e.memset(onesb[:, 2, :], 1.0 / 128), writes=["ones2"])
    A("pool", lambda e: e.memset(onesb[:, 3, :], 1.0), writes=["ones3"])
    A("pool", lambda e: e.memset(S[:], 0.0), writes=[("S", h) for h in range(4)])
    A("pool", lambda e: e.memset(ubuf[:], 0.0), writes=[("ubuf", c) for c in range(4)])
    A("pool", lambda e: e.memset(phalo[:], 0.0), writes=[("phalo", c) for c in range(KC)])
    A("pool", lambda e: e.memset(ptmp[:], 0.0), writes=[("ptmp", i) for i in range(2)])
    A("dve", lambda e: e.tensor_copy(out=identb[:], in_=ident), reads=["cf"], writes=["identb"])
    A("dve", lambda e: e.tensor_copy(out=cmaskb[:], in_=cfc("cmask")), reads=["cf"], writes=["cmaskb"])
    A("act", lambda e: e.activation(out=dv[:, 20:32], in_=cfc("lbp"), func=AF.Exp), reads=["cf"], writes=["dvt"])
    A("dve", lambda e: e.tensor_tensor(out=dv[:, 0:4], in0=dv[:, 20:24], in1=dv[:, 24:28], op=ALU.add), reads=["dvt"], writes=["dv0"])
    A("dve", lambda e: e.tensor_tensor(out=dv[:, 0:4], in0=dv[:, 0:4], in1=dv[:, 28:32], op=ALU.add), reads=["dvt", "dv0"], writes=["dv0"])
    A("dve", lambda e: e.reciprocal(out=dv[:, 4:8], in_=dv[:, 0:4]), reads=["dv0"], writes=["dv1"])
    A("dve", lambda e: e.tensor_tensor(out=dv[:, 0:4], in0=dv[:, 20:24], in1=dv[:, 4:8], op=ALU.mult), reads=["dvt", "dv1", "dv0"], writes=["dv0"])
    A("dve", lambda e: e.tensor_scalar(out=dv[:, 4:8], in0=dv[:, 0:4], scalar1=-1.0, scalar2=1.0, op0=ALU.mult, op1=ALU.add), reads=["dv0", "dv1"], writes=["dv1"])
    A("dve", lambda e: e.tensor_scalar(out=dv[:, 8:12], in0=dv[:, 0:4], scalar1=-1.0, scalar2=None, op0=ALU.add), reads=["dv0"], writes=["dv2"])
    A("dve", lambda e: e.tensor_scalar(out=dv[:, 12:20], in0=cfc("pscale"), scalar1=1.0 / ALPHA, scalar2=None, op0=ALU.mult), reads=["cf"], writes=["dv3"])
    LNN = ["lnmix", "lnxa", "lnffn"]
    for i_, nm in enumerate(LNN):
        for l in range(2):
            o_ = 32 + 16 * (i_ * 2 + l)
            A("dve", lambda e, o_=o_, nm=nm, l=l: e.tensor_scalar(out=dv[:, o_:o_ + 8], in0=cfc(nm + "_g", 8 * l, 8), scalar1=ALPHA, scalar2=None, op0=ALU.mult), reads=["cf"], writes=[("dvln", o_)])
            A("dve", lambda e, o_=o_, nm=nm, l=l: e.tensor_scalar(out=dv[:, o_ + 8:o_ + 16], in0=cfc(nm + "_b", 8 * l, 8), scalar1=ALPHA, scalar2=None, op0=ALU.mult), reads=["cf"], writes=[("dvln", o_ + 8)])
    DVK = ["dv0", "dv1", "dv2", "dv3"] + [("dvln", 32 + 8 * i) for i in range(12)]

    def ln_cols(nm, l, scaled):
        if scaled:
            o_ = 32 + 16 * (LNN.index(nm) * 2 + l)
            return (lambda c: dv[:, o_ + c:o_ + c + 1]), (lambda c: dv[:, o_ + 8 + c:o_ + 9 + c])
        return (lambda c: cfc(nm + "_g", 8 * l + c, 1)), (lambda c: cfc(nm + "_b", 8 * l + c, 1))

    def ln_stats(nch, N, zc, zkeys, ones_idx, eps, s, zb_done=False, pool="L", defer=False):
        bs, bq = pbank(pool), pbank(pool)
        K12, K13 = ("TFL", s, 0), ("TFL", s, 1)
        for c in range(nch):
            if not zb_done:
                A("dve", lambda e, c=c: e.tensor_copy(out=G[:, c, 0:N], in_=zc(c)), reads=[zkeys[c]], writes=[KG(c)])
            A("act", lambda e, c=c: e.activation(out=G[:, 8 + c, 0:N], in_=zc(c), func=AF.Square), reads=[zkeys[c]], writes=[KG(8 + c)])
            A("pe", lambda e, c=c: e.matmul(ps[bs][:, 0:N], lhsT=onesb[:, ones_idx, :], rhs=G[:, c, 0:N], start=(c == 0), stop=(c == nch - 1)),
              reads=[KG(c), f"ones{ones_idx}"], writes=[KP(bs)])
            A("pe", lambda e, c=c: e.matmul(ps[bq][:, 0:N], lhsT=onesb[:, ones_idx, :], rhs=G[:, 8 + c, 0:N], start=(c == 0), stop=(c == nch - 1)),
              reads=[KG(8 + c), f"ones{ones_idx}"], writes=[KP(bq)])
        t0 = TFL[:, s, 0, 0:N]
        t1 = TFL[:, s, 1, 0:N]
        if defer:
            dq["on"] = True
            dq["stream"] = s
        A("act", lambda e: e.activation(out=t0, in_=ps[bs][:, 0:N], func=AF.Square), reads=[KP(bs)], writes=[K12])
        A("dve", lambda e: e.scalar_tensor_tensor(out=t0, in0=t0, scalar=-1.0, in1=ps[bq][:, 0:N], op0=ALU.mult, op1=ALU.add), reads=[K12, KP(bq)], writes=[K12])
        A("dve", lambda e: e.tensor_scalar(out=t0, in0=t0, scalar1=float(eps), scalar2=None, op0=ALU.add), reads=[K12], writes=[K12])
        A("act", lambda e: e.activation(out=t1, in_=t0, func=AF.Ln), reads=[K12], writes=[K13])
        A("act", lambda e: e.activation(out=t1, in_=t1, func=AF.Exp, scale=-0.5), reads=[K13], writes=[K13])
        A("dve", lambda e: e.scalar_tensor_tensor(out=t0, in0=ps[bs][:, 0:N], scalar=-1.0, in1=t1, op0=ALU.mult, op1=ALU.mult), reads=[KP(bs), K13, K12], writes=[K12])
        return t1, t0

    def ln_norm(nch, N, zc, zkeys, rstd, nmr, s):
        K12, K13 = ("TFL", s, 0), ("TFL", s, 1)
        for c in range(nch):
            A("dve", lambda e, c=c: e.tensor_tensor(out=zc(c), in0=zc(c), in1=rstd, op=ALU.mult), reads=[zkeys[c], K13], writes=[zkeys[c]])
            A("dve", lambda e, c=c: e.tensor_tensor(out=zc(c), in0=zc(c), in1=nmr, op=ALU.add), reads=[zkeys[c], K12], writes=[zkeys[c]])

    def ln_stream(nm, l, s, final=False):
        zc = lambda c: xres[:, s, c, :]
        zk = [KX(s, c) for c in range(KC)]
        flush()
        rstd, nmr = ln_stats(KC, T, zc, zk, 0, LN_EPS, s, zb_done=True, defer=True)
        ln_norm(KC, T, zc, zk, rstd, nmr, s)
        g1, b1 = ln_cols(nm, l, False)
        g2, b2 = ln_cols(nm, l, not final)
        for c in range(KC):
            A("act", lambda e, c=c: e.activation(out=xb[:, s, c, :], in_=xres[:, s, c, :], func=AF.Identity, scale=g1(c), bias=b1(c)),
              reads=[KX(s, c), "cf"], writes=[KB(s, c)])
            A("act", lambda e, c=c: e.activation(out=xres[:, s, c, :], in_=xres[:, s, c, :], func=AF.Identity, scale=g2(c), bias=b2(c)),
              reads=[KX(s, c), "cf"] + DVK, writes=[KX(s, c)])
        if final and final > 0:
            store_out(final - 1, s)
        dq["on"] = False

    def evac_resid(s, scale=None):
        def ev(i, oc, b):
            if scale is None:
                A("dve", lambda e: e.tensor_tensor(out=xres[:, s, oc, :], in0=ps[b][:], in1=xres[:, s, oc, :], op=ALU.add), reads=[KP(b), KX(s, oc)], writes=[KX(s, oc)])
            else:
                A("dve", lambda e: e.scalar_tensor_tensor(out=xres[:, s, oc, :], in0=ps[b][:], scalar=scale(oc), in1=xres[:, s, oc, :], op0=ALU.mult, op1=ALU.add),
                  reads=[KP(b), KX(s, oc)] + DVK, writes=[KX(s, oc)])
        return ev

    def zb_cast(s):
        for c in range(KC):
            A("dve", lambda e, c=c: e.tensor_copy(out=G[:, c, :], in_=xres[:, s, c, :]), reads=[KX(s, c)], writes=[KG(c)])

    iost = {"n": 0}

    def load_x(src, row0, need_res, s):
        for tt in range(4):
            r = iost["n"] % 2
            iost["n"] += 1
            pg.add("pool", lambda e, sem, r=r, tt=tt: e.dma_start(out=io[:, r, :], in_=src[row0 + tt * P:row0 + (tt + 1) * P, :]).then_inc(sem, 16),
                   writes=[("io", r)], dma=f"io{r}")
            for half in range(2):
                b = pbank("C")

                def fn(e, b=b, r=r, half=half):
                    ins = None
                    for j in range(4):
                        c = half * 4 + j
                        ins = e.transpose(ps[b][:, j * P:(j + 1) * P], io[:, r, c * P:(c + 1) * P], ident)
                    return ins
                A("pe", fn, reads=[("io", r), "cf"], writes=[KP(b)])
                src_v = ps[b][:].rearrange("p (j t) -> p j t", j=4)
                sl_ = slice(tt * P, (tt + 1) * P)
                cs = slice(half * 4, half * 4 + 4)
                A("dve", lambda e, src_v=src_v, cs=cs, sl_=sl_: e.tensor_copy(out=xb[:, s, cs, sl_], in_=src_v),
                  reads=[KP(b)], writes=[KB(s, c) for c in range(half * 4, half * 4 + 4)])
                if need_res:
                    A("act", lambda e, src_v=src_v, cs=cs, sl_=sl_: e.mul(out=xres[:, s, cs, sl_], in_=src_v, mul=float(ALPHA)),
                      reads=[KP(b)], writes=[KX(s, c) for c in range(half * 4, half * 4 + 4)])

    out_ops = []

    def store_out(row0, s):
        for tt in range(4):
            r = 2
            for half in range(2):
                b = pbank("L")

                def fn(e, b=b, half=half, tt=tt):
                    ins = None
                    for j in range(4):
                        c = half * 4 + j
                        ins = e.transpose(ps[b][:, j * P:(j + 1) * P], xres[:, s, c, tt * P:(tt + 1) * P], ident)
                    return ins
                A("pe", fn, reads=[KX(s, c) for c in range(half * 4, half * 4 + 4)] + ["cf"], writes=[KP(b)])
                eng = "act" if half == 0 else "dve"
                if eng == "act":
                    A("act", lambda e, b=b, r=r, half=half: e.copy(out=io[:, r, half * T:(half + 1) * T], in_=ps[b][:]), reads=[KP(b)], writes=[("io", r)])
                else:
                    A("dve", lambda e, b=b, r=r, half=half: e.tensor_copy(out=io[:, r, half * T:(half + 1) * T], in_=ps[b][:]), reads=[KP(b)], writes=[("io", r)])
            A("pool", lambda e, sem, r=r, tt=tt: e.dma_start(out=outd[row0 + tt * P:row0 + (tt + 1) * P, :], in_=io[:, r, :]).then_inc(sem, 16),
              reads=[("io", r)], writes=[("out", row0, tt)], dma=f"io{r}")
            out_ops.append(("out", row0, tt))

    def ffn(l, s, store_row0=None):
        xk = [KB(s, c) for c in range(KC)]
        for j0 in range(0, NJ, 2):
            slg, vg = get_panel(f"wg{l}", 0, KC, j0 * P, 2 * P)
            slu, vu = get_panel(f"wu{l}", 0, KC, j0 * P, 2 * P)
            for jj in range(2):
                j = j0 + jj
                bg, bu = pbank("A"), pbank("A")

                def fng(e, bg=bg, vg=vg, jj=jj):
                    ins = None
                    for kc in range(KC):
                        ins = e.matmul(ps[bg][:], lhsT=vg[:, kc, jj * P:(jj + 1) * P], rhs=xb[:, s, kc, :], start=(kc == 0), stop=(kc == KC - 1))
                    return ins

                def fnu(e, bu=bu, vu=vu, jj=jj):
                    ins = None
                    for kc in range(KC):
                        ins = e.matmul(ps[bu][:], lhsT=vu[:, kc, jj * P:(jj + 1) * P], rhs=xb[:, s, kc, :], start=(kc == 0), stop=(kc == KC - 1))
                    return ins
                A("pe", fng, reads=[("w", slg)] + xk, writes=[KP(bg)])
                A("pe", fnu, reads=[("w", slu)] + xk, writes=[KP(bu)])
                tix = (10, 0)[j % 2]
                A("act", lambda e, bg=bg, tix=tix: e.activation(out=TF[:, tix, :], in_=ps[bg][:], func=AF.Silu), reads=[KP(bg)], writes=[KT_(tix)])
                A("dve", lambda e, bu=bu, tix=tix, j=j: e.tensor_tensor(out=G[:, j, :], in0=ps[bu][:], in1=TF[:, tix, :], op=ALU.mult),
                  reads=[KP(bu), KT_(tix)], writes=[KG(j)])
        ev = evac_resid(s)
        for c in range(KC):
            b = pbank("A")
            for hf in range(2):
                sl, view = get_panel(f"wd{l}", hf * 11, 11, c * P, P)

                def fn(e, b=b, view=view, hf=hf):
                    ins = None
                    for k in range(11):
                        j = hf * 11 + k
                        ins = e.matmul(ps[b][:], lhsT=view[:, k, :], rhs=G[:, j, :], start=(j == 0), stop=(j == NJ - 1))
                    return ins
                A("pe", fn, reads=[("w", sl)] + [KG(hf * 11 + k) for k in range(11)], writes=[KP(b)])
            ev(c, c, b)
        zb_cast(s)
        ln_stream("lnffn", l, s, final=(0 if l == 0 else (store_row0 + 1 if store_row0 is not None else -1)))

    def xattn(l, s):
        xk = [KB(s, c) for c in range(KC)]

        def evq(i, oc, b):
            A("act", lambda e: e.mul(out=G[:, oc, :], in_=ps[b][:], mul=1.0 / 16.0), reads=[KP(b)], writes=[KG(oc)])
        proj_fm(f"wq{l}", KC, list(range(KC)), lambda kc: xb[:, s, kc, :], xk, evq)
        for h in range(4):
            pk = [16 + (h % 2) * 2, 17 + (h % 2) * 2]
            for mc in range(2):
                b = pbank("C")

                def fn(e, b=b, h=h, mc=mc):
                    ins = None
                    for dc in range(2):
                        ins = e.matmul(ps[b][:], lhsT=KT[:, l, 2 * h + dc, mc * P:(mc + 1) * P], rhs=G[:, 2 * h + dc, :], start=(dc == 0), stop=(dc == 1))
                    return ins
                A("pe", fn, reads=[("KT", l), KG(2 * h), KG(2 * h + 1)], writes=[KP(b)])
                A("act", lambda e, b=b, g_=pk[mc]: e.activation(out=G[:, g_, :], in_=ps[b][:], func=AF.Exp), reads=[KP(b)], writes=[KG(pk[mc])])
            bsum = pbank("B")

            def fns(e, bsum=bsum, pk=pk):
                ins = None
                for mc in range(2):
                    ins = e.matmul(ps[bsum][:], lhsT=onesb[:, 3, :], rhs=G[:, pk[mc], :], start=(mc == 0), stop=(mc == 1))
                return ins
            A("pe", fns, reads=[KG(pk[0]), KG(pk[1]), "ones3"], writes=[KP(bsum)])
            tix = (10, 0)[h % 2]
            A("act", lambda e, bsum=bsum, tix=tix: e.activation(out=TF[:, tix, :], in_=ps[bsum][:], func=AF.Ln), reads=[KP(bsum)], writes=[KT_(tix)])
            A("act", lambda e, tix=tix: e.activation(out=TF[:, tix, :], in_=TF[:, tix, :], func=AF.Exp, scale=-1.0), reads=[KT_(tix)], writes=[KT_(tix)])
            for dc in range(2):
                b = pbank("A")

                def fno(e, b=b, h=h, dc=dc, pk=pk):
                    ins = None
                    for mc in range(2):
                        ins = e.matmul(ps[b][:], lhsT=Vm[:, l, mc, (2 * h + dc) * P:(2 * h + dc + 1) * P], rhs=G[:, pk[mc], :], start=(mc == 0), stop=(mc == 1))
                    return ins
                A("pe", fno, reads=[("Vm", l), KG(pk[0]), KG(pk[1])], writes=[KP(b)])
                A("dve", lambda e, b=b, tix=tix, oc=2 * h + dc: e.tensor_tensor(out=G[:, 8 + oc, :], in0=ps[b][:], in1=TF[:, tix, :], op=ALU.mult),
                  reads=[KP(b), KT_(tix)], writes=[KG(8 + 2 * h + dc)])
        proj_fm(f"wo{l}", KC, list(range(KC)), lambda kc: G[:, 8 + kc, :], [KG(8 + c) for c in range(KC)], evac_resid(s))
        zb_cast(s)
        ln_stream("lnxa", l, s)

    def mem_kv():
        memT = lambda c: TF[:, c // 2, (c % 2) * MEM:(c % 2 + 1) * MEM]
        mk = [("memT", c) for c in range(KC)]
        for mt in range(2):
            r = iost["n"] % 2
            iost["n"] += 1
            pg.add("pool", lambda e, sem, r=r, mt=mt: e.dma_start(out=io[:, r, :], in_=memd[mt * P:(mt + 1) * P, :]).then_inc(sem, 16), writes=[("io", r)], dma=f"io{r}")
            for half in range(2):
                b = pbank("C")

                def fn(e, b=b, r=r, half=half):
                    ins = None
                    for j in range(4):
                        c = half * 4 + j
                        ins = e.transpose(ps[b][:, j * P:(j + 1) * P], io[:, r, c * P:(c + 1) * P], ident)
                    return ins
                A("pe", fn, reads=[("io", r), "cf"], writes=[KP(b)])
                for j in range(4):
                    c = half * 4 + j
                    A("dve", lambda e, b=b, j=j, c=c, mt=mt: e.tensor_copy(out=memT(c)[:, mt * P:(mt + 1) * P], in_=ps[b][:, j * P:(j + 1) * P]),
                      reads=[KP(b)], writes=[mk[c], KT_(c // 2)])
        rstd, nmr = ln_stats(KC, MEM, memT, mk, 0, LN_EPS, 0)
        ln_norm(KC, MEM, memT, mk, rstd, nmr, 0)
        mn = lambda c: G[:, 16 + c // 2, (c % 2) * MEM:(c % 2 + 1) * MEM]
        mnk = [KG(16 + c // 2) for c in range(KC)]
        for c in range(KC):
            A("dve", lambda e, c=c: e.tensor_scalar(out=mn(c), in0=memT(c), scalar1=cfc("memg", c, 1), scalar2=cfc("memb", c, 1), op0=ALU.mult, op1=ALU.add),
              reads=[mk[c], "cf"], writes=[mnk[c]])
        for l in range(2):
            def evk(i, oc, b, l=l):
                A("act", lambda e: e.copy(out=KT[:, l, oc, :], in_=ps[b][:, 0:MEM]), reads=[KP(b)], writes=[("KT", l)])
            proj_fm(f"wk{l}", KC, list(range(KC)), mn, list(set(mnk)), evk, N=MEM)
            for cp in range(4):
                sl, view = get_panel(f"wv{l}", 0, KC, cp * 256, 256, to_scratch=False)
                for mc in range(2):
                    b = pbank("A")

                    def fn(e, b=b, view=view, mc=mc):
                        ins = None
                        for kc in range(KC):
                            ins = e.matmul(ps[b][:, 0:256], lhsT=mn(kc)[:, mc * P:(mc + 1) * P], rhs=view[:, kc, :], start=(kc == 0), stop=(kc == KC - 1))
                        return ins
                    A("pe", fn, reads=[("w", sl)] + list(set(mnk)), writes=[KP(b)])
                    A("act", lambda e, b=b, mc=mc, cp=cp, l=l: e.copy(out=Vm[:, l, mc, cp * 256:(cp + 1) * 256], in_=ps[b][:, 0:256]), reads=[KP(b)], writes=[("Vm", l)])

    def mix1_chain(first_group, s):
        dq["on"] = True
        dq["stream"] = s
        for c in range(KC):
            gi = c // 2
            w = 2 << gi
            nsteps = gi + 1
            pb0 = 0
            bufs = [("ptmp", pb0), ("ptmp", pb0 + 1)]
            A("dve", lambda e, c=c, pb0=pb0: e.tensor_copy(out=ptmp[:, pb0, 0:16], in_=phalo[:, c, :]), reads=[("phalo", c)], writes=[bufs[0]])
            A("act", lambda e, c=c, pb0=pb0: e.copy(out=ptmp[:, pb0, 16:16 + T], in_=xres[:, s, c, :]), reads=[KX(s, c)], writes=[bufs[0]])
            A("dve", lambda e, c=c: e.tensor_copy(out=phalo[:, c, :], in_=xres[:, s, c, T - 16:T]), reads=[KX(s, c), bufs[0]], writes=[("phalo", c)])
            cur = 0
            sh = 1
            for s_ in range(nsteps):
                nxt = 1 - cur
                A("dve", lambda e, cur=cur, nxt=nxt, sh=sh, pb0=pb0: e.tensor_tensor(out=ptmp[:, pb0 + nxt, sh:16 + T], in0=ptmp[:, pb0 + cur, sh:16 + T], in1=ptmp[:, pb0 + cur, 0:16 + T - sh], op=ALU.add),
                  reads=[bufs[cur]], writes=[bufs[nxt]])
                cur = nxt
                sh *= 2
            A("dve", lambda e, cur=cur, c=c, w=w, pb0=pb0: e.scalar_tensor_tensor(out=P8[:, c, :], in0=ptmp[:, pb0 + cur, 16:16 + T], scalar=1.0 / w, in1=xres[:, s, c, :], op0=ALU.mult, op1=ALU.subtract),
              reads=[bufs[cur], KX(s, c)], writes=[("P8", c)])
            if first_group:
                A("dve", lambda e, cur=cur, gi=gi, pb0=pb0: e.tensor_tensor(out=ptmp[:, pb0 + cur, 16:32], in0=ptmp[:, pb0 + cur, 16:32], in1=cfc("invcnt", gi * 16, 16), op=ALU.mult),
                  reads=[bufs[cur], "cf"], writes=[bufs[cur]])
                A("dve", lambda e, cur=cur, c=c, pb0=pb0: e.tensor_tensor(out=P8[:, c, 0:16], in0=ptmp[:, pb0 + cur, 16:32], in1=xres[:, s, c, 0:16], op=ALU.subtract),
                  reads=[bufs[cur], KX(s, c)], writes=[("P8", c)])
        dq["on"] = False

    def mix1(first_group, s):
        ev = evac_resid(s, scale=lambda oc: dv[:, 12 + oc:13 + oc])
        for gi in range(4):
            sl, view = get_panel("pw", 2 * gi, 2, 0, 256)
            for oo in range(2):
                oc = 2 * gi + oo
                b = pbank("A")

                def fn(e, b=b, view=view, gi=gi, oo=oo):
                    ins = None
                    for kc in range(2):
                        ins = e.matmul(ps[b][:], lhsT=view[:, kc, oo * P:(oo + 1) * P], rhs=P8[:, 2 * gi + kc, :], start=(kc == 0), stop=(kc == 1))
                    return ins
                A("pe", fn, reads=[("w", sl), ("P8", 2 * gi), ("P8", 2 * gi + 1)], writes=[KP(b)])
                ev(oc, oc, b)
        zb_cast(s)
        ln_stream("lnmix", 1, s)

    sbst = {"n": 0}

    def mix0(need_out, need_u, s):
        xk = [KB(s, c) for c in range(KC)]
        act = lambda kc: xb[:, s, kc, :]
        for cp in range(2):
            sl, view = get_panel("win", 0, KC, 1024 + cp * 256, 256)
            for tt in range(4):
                b = pbank("A")

                def fn(e, b=b, view=view, tt=tt):
                    ins = None
                    for kc in range(KC):
                        ins = e.matmul(ps[b][:, 0:256], lhsT=xb[:, s, kc, tt * P:(tt + 1) * P], rhs=view[:, kc, :], start=(kc == 0), stop=(kc == KC - 1))
                    return ins
                A("pe", fn, reads=[("w", sl)] + xk, writes=[KP(b)])
                A("act", lambda e, b=b, tt=tt, cp=cp: e.copy(out=vT[:, tt, cp * 256:(cp + 1) * 256], in_=ps[b][:, 0:256]), reads=[KP(b)], writes=[("vT", tt)])
        def evf(i, h, b):
            sig, logf, kk, bb, eb, enb, blb = [TF[:, t_, :] for t_ in range(7)]
            A("act", lambda e: e.activation(out=sig, in_=ps[b][:], func=AF.Exp, scale=-1.0), reads=[KP(b)], writes=[KT_(0)])
            A("act", lambda e: e.activation(out=logf, in_=sig, func=AF.Ln, scale=dv[:, h:h + 1], bias=1.0), reads=[KT_(0)] + DVK, writes=[KT_(1)])
            A("act", lambda e: e.activation(out=enb, in_=sig, func=AF.Ln, bias=1.0), reads=[KT_(0)], writes=[KT_(5)])
            A("dve", lambda e: e.tensor_tensor(out=logf, in0=logf, in1=enb, op=ALU.subtract), reads=[KT_(1), KT_(5)], writes=[KT_(1)])
            A("act", lambda e: e.activation(out=enb, in_=enb, func=AF.Exp, scale=-1.0), reads=[KT_(5)], writes=[KT_(5)])
            A("dve", lambda e: e.scalar_tensor_tensor(out=kk, in0=sig, scalar=dv[:, 4 + h:5 + h], in1=enb, op0=ALU.mult, op1=ALU.mult), reads=[KT_(0), KT_(5)] + DVK, writes=[KT_(2)])
            A("dve", lambda e: e.tensor_tensor_scan(out=bb, data0=cfc("scanmask"), data1=logf, initial=0.0, op0=ALU.mult, op1=ALU.add), reads=[KT_(1), "cf"], writes=[KT_(3)])
            bv = bb.rearrange("p (n c) -> p n c", c=32)
            A("dve", lambda e: e.tensor_tensor(out=blb.rearrange("p (n c) -> p n c", c=32), in0=bv[:, :, 31:32].to_broadcast([P, 16, 32]), in1=bv, op=ALU.subtract),
              reads=[KT_(3)], writes=[KT_(6)])
            A("act", lambda e: e.activation(out=eb, in_=bb, func=AF.Exp), reads=[KT_(3)], writes=[KT_(4)])
            A("act", lambda e: e.activation(out=blb, in_=blb, func=AF.Exp), reads=[KT_(6)], writes=[KT_(6)])
            A("dve", lambda e: e.tensor_copy(out=dcy[:, h, :], in_=eb.rearrange("p (n c) -> p n c", c=32)[:, :, 31]), reads=[KT_(4)], writes=[("dcy", h)])
            A("dve", lambda e: e.tensor_tensor(out=G[:, 8 + h, :], in0=kk, in1=blb, op=ALU.mult), reads=[KT_(2), KT_(6)], writes=[KG(8 + h)])
            if need_out:
                A("act", lambda e: e.activation(out=enb, in_=bb, func=AF.Exp, scale=-1.0), reads=[KT_(3)], writes=[KT_(5)])
                A("dve", lambda e: e.tensor_tensor(out=G[:, 4 + h, :], in0=kk, in1=enb, op=ALU.mult), reads=[KT_(2), KT_(5)], writes=[KG(4 + h)])
                def evq(i2, oc2, b2):
                    A("dve", lambda e: e.tensor_tensor(out=G[:, h, :], in0=ps[b2][:], in1=eb, op=ALU.mult), reads=[KP(b2), KT_(4)], writes=[KG(h)])
                if h % 2 == 0:
                    qpan["p"] = get_panel("win", 0, KC, h * P, 2 * P)
                slq, vq = qpan["p"]
                bq_ = pbank("A")

                def fnq(e, bq_=bq_, vq=vq, gi_=h % 2):
                    ins = None
                    for kc in range(KC):
                        ins = e.matmul(ps[bq_][:], lhsT=vq[:, kc, gi_ * P:(gi_ + 1) * P], rhs=xb[:, s, kc, :], start=(kc == 0), stop=(kc == KC - 1))
                    return ins
                A("pe", fnq, reads=[("w", slq)] + xk, writes=[KP(bq_)])
                evq(0, h, bq_)
        qpan = {}
        proj_fm("win", KC, [4 + h for h in range(4)], act, xk, lambda i, oc, b: evf(i, oc - 4, b), cols_per_panel=256)
        pieces = []
        if need_out:
            def evg(i, oc, b):
                h = oc - 12
                A("act", lambda e: e.activation(out=TF[:, 7, :], in_=ps[b][:], func=AF.Silu), reads=[KP(b)], writes=[KT_(7)])
                A("dve", lambda e: e.tensor_scalar(out=G[:, 12 + h, :], in0=TF[:, 7, :], scalar1=cfc("hgn", h, 1), scalar2=None, op0=ALU.mult), reads=[KT_(7), "cf"], writes=[KG(12 + h)])
            pieces.append(lambda: proj_fm("win", KC, [12, 13], act, xk, evg))
            pieces.append(lambda: proj_fm("win", KC, [14, 15], act, xk, evg))
        def cpiece(cc):
            bga = pbank("A")
            sla, va = get_panel("win", 0, KC, (16 + cc) * P, P)
            slg, vg = get_panel("win", 0, KC, (20 + cc) * P, P)
            bgg = pbank("A")

            def fna(e, b=bga, v=va):
                ins = None
                for kc in range(KC):
                    ins = e.matmul(ps[b][:], lhsT=v[:, kc, :], rhs=xb[:, s, kc, :], start=(kc == 0), stop=(kc == KC - 1))
                return ins

            def fngt(e, b=bgg, v=vg):
                ins = None
                for kc in range(KC):
                    ins = e.matmul(ps[b][:], lhsT=v[:, kc, :], rhs=xb[:, s, kc, :], start=(kc == 0), stop=(kc == KC - 1))
                return ins
            A("pe", fna, reads=[("w", sla)] + xk, writes=[KP(bga)])
            A("pe", fngt, reads=[("w", slg)] + xk, writes=[KP(bgg)])
            A("act", lambda e, b=bgg: e.activation(out=TF[:, 7, :], in_=ps[b][:], func=AF.Sigmoid), reads=[KP(bgg)], writes=[KT_(7)])
            A("dve", lambda e, b=bga, cc=cc: e.tensor_tensor(out=ubuf[:, cc, HALO:HALO + T], in0=ps[b][:], in1=TF[:, 7, :], op=ALU.mult), reads=[KP(bga), KT_(7)], writes=[("ubuf", cc)])
            if need_out:
                A("pool", lambda e, cc=cc: e.tensor_tensor(out=diag[:], in0=identb[:].unsqueeze(1).to_broadcast([P, CVK, P]),
                                                           in1=cfc("cvw", cc * CVK, CVK).unsqueeze(2).to_broadcast([P, CVK, P]), op=ALU.mult),
                  reads=["identb", "cf"], writes=["diag"])
                bc = pbank("A")

                def fnc(e, bc=bc, cc=cc):
                    ins = None
                    for j in range(CVK):
                        ins = e.matmul(ps[bc][:], lhsT=diag[:, j, :], rhs=ubuf[:, cc, j:j + T], start=(j == 0), stop=(j == CVK - 1))
                    return ins
                A("pe", fnc, reads=["diag", ("ubuf", cc)], writes=[KP(bc)])
                A("act", lambda e, bc=bc, cc=cc: e.activation(out=TF[:, cc, :], in_=ps[bc][:], func=AF.Identity, bias=cfc("cvb", cc, 1)), reads=[KP(bc), "cf"], writes=[KT_(cc)])
            A("dve", lambda e, cc=cc: e.tensor_copy(out=ubuf[:, cc, 0:HALO], in_=ubuf[:, cc, T:T + HALO]), reads=[("ubuf", cc)], writes=[("ubuf", cc)])
        if need_u or need_out:
            for cc in range(4):
                pieces.append(lambda cc=cc: cpiece(cc))
        def chain(tt):
            ts_ = slice(tt * P, (tt + 1) * P)
            bt = pbank("A")
            ptv = ps[bt][:].bitcast(BF16)

            def fnt(e, ptv=ptv, ts_=ts_):
                ins = None
                for h in range(4):
                    ins = e.transpose(ptv[:, h * P:(h + 1) * P], G[:, 8 + h, ts_], identb[:])
                return ins
            A("pe", fnt, reads=[KG(8 + h) for h in range(4)] + ["identb"], writes=[KP(bt)])
            A("act", lambda e, ptv=ptv, tt=tt: e.copy(out=kendT[:, tt, :], in_=ptv[:, 0:T]), reads=[KP(bt)], writes=[("kendT", tt)])
            for n in range(4):
                si = (tt % 2) * 4 + n
                if need_out:
                    A("act", lambda e, si=si: e.copy(out=Sb[:, si, :, :], in_=S[:]), reads=[("S", h) for h in range(4)], writes=[("Sb", si)])
                bd = pbank("A")

                def fnd(e, bd=bd, n=n, tt=tt):
                    ins = None
                    for h in range(4):
                        ins = e.matmul(ps[bd][:, h * P:(h + 1) * P], lhsT=kendT[n * 32:(n + 1) * 32, tt, h * P:(h + 1) * P], rhs=vT[n * 32:(n + 1) * 32, tt, h * P:(h + 1) * P],
                                       start=True, stop=True, tile_position=(n * 32, 0))
                    return ins
                A("pe", fnd, reads=[("kendT", tt), ("vT", tt)], writes=[KP(bd)])
                for h in range(4):
                    A("dve", lambda e, bd=bd, h=h, cn=tt * 4 + n: e.scalar_tensor_tensor(out=S[:, h, :], in0=S[:, h, :], scalar=dcy[:, h, cn:cn + 1], in1=ps[bd][:, h * P:(h + 1) * P], op0=ALU.mult, op1=ALU.add),
                      reads=[("S", h), KP(bd), ("dcy", h)], writes=[("S", h)])

        def outs(tt):
            ts_ = slice(tt * P, (tt + 1) * P)
            bsc = pbank("C")

            def fnsc(e, bsc=bsc, ts_=ts_):
                ins = None
                for h in range(4):
                    ins = e.matmul(ps[bsc][:, h * P:(h + 1) * P], lhsT=G[:, 4 + h, ts_], rhs=G[:, h, ts_], start=True, stop=True)
                return ins
            A("pe", fnsc, reads=[KG(h) for h in range(8)], writes=[KP(bsc)])
            scm = TF[:, 8, :].bitcast(BF16)[:, 0:T]
            A("dve", lambda e, bsc=bsc, scm=scm: e.tensor_tensor(out=scm.rearrange("p (h t) -> p h t", h=4), in0=ps[bsc][:].rearrange("p (h t) -> p h t", h=4),
                                                              in1=cmaskb[:].unsqueeze(1).to_broadcast([P, 4, P]), op=ALU.mult),
              reads=[KP(bsc), "cmaskb"], writes=[KT_(8)])
            bo = pbank("C")

            def fno(e, bo=bo, tt=tt):
                ins = None
                for h in range(4):
                    ins = e.matmul(ps[bo][:, h * P:(h + 1) * P], lhsT=vT[:, tt, h * P:(h + 1) * P], rhs=scm[:, h * P:(h + 1) * P], start=(h == 0), stop=False, skip_group_check=True)
                for n in range(4):
                    si = (tt % 2) * 4 + n
                    for h in range(4):
                        ins = e.matmul(ps[bo][:, h * P + n * 32:h * P + (n + 1) * 32], lhsT=Sb[:, si, h, :], rhs=G[:, h, tt * P + n * 32:tt * P + (n + 1) * 32],
                                       start=False, stop=(n == 3), skip_group_check=True)
                return ins
            A("pe", fno, reads=[("vT", tt), KT_(8)] + [("Sb", (tt % 2) * 4 + n) for n in range(4)] + [KG(h) for h in range(4)], writes=[KP(bo)])
            A("act", lambda e, bo=bo: e.copy(out=TF[:, 9, :], in_=ps[bo][:]), reads=[KP(bo)], writes=[KT_(9)])
            A("act", lambda e: e.activation(out=G[:, 20, :], in_=TF[:, 9, :], func=AF.Square), reads=[KT_(9)], writes=[KG(20)])
            bss = pbank("B")
            A("pe", lambda e, bss=bss: e.matmul(ps[bss][:], lhsT=onesb[:, 2, :], rhs=G[:, 20, :], start=True, stop=True), reads=[KG(20), "ones2"], writes=[KP(bss)])
            A("dve", lambda e, bss=bss: e.tensor_scalar(out=TF[:, 10, :], in0=ps[bss][:], scalar1=float(RMS_EPS), scalar2=None, op0=ALU.add), reads=[KP(bss)], writes=[KT_(10)])
            A("act", lambda e: e.activation(out=TF[:, 10, :], in_=TF[:, 10, :], func=AF.Ln), reads=[KT_(10)], writes=[KT_(10)])
            A("act", lambda e: e.activation(out=TF[:, 10, :], in_=TF[:, 10, :], func=AF.Exp, scale=-0.5), reads=[KT_(10)], writes=[KT_(10)])
            A("dve", lambda e: e.tensor_tensor(out=TF[:, 9, :], in0=TF[:, 9, :], in1=TF[:, 10, :], op=ALU.mult), reads=[KT_(9), KT_(10)], writes=[KT_(9)])
            A("dve", lambda e, ts_=ts_: e.tensor_tensor(out=G[:, 16:20, ts_], in0=TF[:, 9, :].rearrange("p (h t) -> p h t", h=4), in1=G[:, 12:16, ts_], op=ALU.mult),
              reads=[KT_(9)] + [KG(12 + h) for h in range(4)], writes=[KG(16 + h) for h in range(4)])

        chain(0)
        for tt in range(4):
            if need_out and len(pieces) > 4:
                pieces.pop(0)()
            if tt + 1 < 4:
                chain(tt + 1)
            if pieces:
                pieces.pop(0)()
            if need_out:
                outs(tt)
        while pieces:
            pieces.pop(0)()
        if need_out:
            zc = lambda c: TF[:, c, :]
            zk = [KT_(c) for c in range(4)]
            rstd, nmr = ln_stats(4, T, zc, zk, 1, LN_EPS, s, pool="C")
            ln_norm(4, T, zc, zk, rstd, nmr, s)
            for cc in range(4):
                A("act", lambda e, cc=cc: e.activation(out=G[:, 20 + cc, :], in_=TF[:, cc, :], func=AF.Silu, scale=cfc("cvg", cc, 1), bias=cfc("cvbeta", cc, 1)),
                  reads=[KT_(cc), "cf"], writes=[KG(20 + cc)])
            proj_fm("wout", KC, list(range(KC)), lambda kc: G[:, 16 + kc, :], [KG(16 + c) for c in range(KC)], evac_resid(s))
            zb_cast(s)
            ln_stream("lnmix", 0, s)

    en = set(enable)
    if "xa0" in en or "xa1" in en:
        mem_kv()
    units = []
    if use_prefix and "mix0" in en:
        for g in range(NG - 1):
            def light(s, g=g):
                load_x(xprev, g * T, False, s)
                mix0(False, g == NG - 2, s)
            light(g % 2)
        def w_first(s):
            load_x(xprev, (NG - 1) * T, True, s)
            mix0(True, True, s)
        wst = [w_first]
        if "xa0" in en:
            wst.append(lambda s: xattn(0, s))

        def w_last(s):
            if "ffn0" in en:
                ffn(0, s)
            flush()
            A("dve", lambda e: e.tensor_scalar(out=phalo[:], in0=xres[:, s, :, T - 16:T], scalar1=cfc("flag"), scalar2=None, op0=ALU.mult),
              reads=[KX(s, c) for c in range(KC)] + ["cf"], writes=[("phalo", c) for c in range(KC)])
        wst.append(w_last)
        units.append(wst)
    for g in range(NG):
        names = []
        for l in range(2):
            for nm_ in ("mix", "xa", "ffn"):
                if f"{nm_}{l}" in en:
                    names.append((nm_, l))

        def mk_stage(g, idx, names):
            def stage(s):
                if idx == 0:
                    load_x(xcur, g * T, True, s)
                if names:
                    nm_, l = names[idx]
                    if nm_ == "mix" and l == 0:
                        mix0(True, True, s)
                    elif nm_ == "mix":
                        mix1(g == 0, s)
                    elif nm_ == "xa":
                        xattn(l, s)
                    elif l == 1:
                        ffn(l, s, store_row0=g * T)
                    else:
                        ffn(l, s)
                if "mix1" in en and idx + 1 < len(names) and names[idx + 1] == ("mix", 1):
                    mix1_chain(g == 0, s)
                if idx == max(len(names) - 1, 0) and "ffn1" not in en:
                    flush()
                    for c in range(KC):
                        A("act", lambda e, c=c: e.mul(out=xres[:, s, c, :], in_=xres[:, s, c, :], mul=1.0 / float(ALPHA)), reads=[KX(s, c)], writes=[KX(s, c)])
                    store_out(g * T, s)
            return stage
        units.append([mk_stage(g, i, names) for i in range(max(len(names), 1))])
    pipelined = not os.environ.get("MK_NOPIPE")
    pending = list(units)
    active = []
    free_streams = [0, 1]

    def refill():
        while pending and len(active) < (2 if pipelined else 1):
            active.append([pending.pop(0), 0, free_streams.pop(0)])
    refill()
    last = None
    while active:
        cand = [e_ for e_ in active if e_ is not last] or active
        ent = cand[0]
        stages_, idx_, sidx_ = ent
        if dq["q"] and dq["stream"] == sidx_:
            flush()
        n0_ = len(pg.ops)
        stages_[idx_](sidx_)
        if os.environ.get("MK_DUMP"):
            print("MKSTAGE stream", sidx_, "stage", idx_, "ops", n0_, "->", len(pg.ops), "queued", len(dq["q"]))
        ent[1] += 1
        last = ent
        if ent[1] == len(stages_):
            active.remove(ent)
            free_streams.append(sidx_)
            refill()
    flush()
    pg.add("sp", lambda e: None, reads=out_ops)
    nmax = int(os.environ.get("MK_MAXOPS", "0"))
    if nmax:
        for i_, op in enumerate(pg.ops[:nmax][-5:]):
            print("MK op", nmax - 5 + i_, op.eng, op.line)
        pg.ops = pg.ops[:nmax]
        sk = os.environ.get("MK_SKIP")
        if sk:
            pg.ops = [o for i_, o in enumerate(pg.ops) if i_ != int(sk)]
    print("MK total ops", len(pg.ops))
    pg.finalize()
    if os.environ.get("MK_DUMP"):
        for i_, op in enumerate(pg.ops):
            print("MKD", i_, op.eng, op.stream, op.sidx, op.line, "sig" if op.signal else "", [(s_, i2, pg.val[s_][i2]) for s_, i2 in op.waits])
    pg.emit(nc, es)
    es.close()
    nc._used_inputs = set(WSHAPE[k][0] for k in W)
    return nc


def make_cf(inputs, half, first_tokens_special):
    cfa = np.zeros((P, NCF), np.float32)

    def put(name, arr):
        o, n = CF_COLS[name]
        assert arr.shape == (P, n), (name, arr.shape, n)
        cfa[:, o:o + n] = arr
    put("ident", np.eye(P, dtype=np.float32))
    sm = np.ones((P, T), np.float32)
    sm[:, ::32] = 0.0
    put("scanmask", sm)
    s_ = np.arange(P)[:, None]
    t_ = np.arange(P)[None, :]
    put("cmask", ((s_ // 32 == t_ // 32) & (s_ <= t_)).astype(np.float32))
    ic = np.zeros((P, 64), np.float32)
    for gi in range(4):
        w = 2 << gi
        pos = np.arange(1, 17, dtype=np.float32)
        if first_tokens_special:
            ic[:, gi * 16:(gi + 1) * 16] = (1.0 / np.minimum(pos, float(w)))[None, :]
        else:
            ic[:, gi * 16:(gi + 1) * 16] = 1.0 / w
    put("invcnt", ic)
    put("flag", np.full((P, 1), float(half), np.float32))
    put("lbp", _vec_cols(inputs["lb_param"]))
    put("memg", _vec_cols(inputs["mem_ln_g"]))
    put("memb", _vec_cols(inputs["mem_ln_b"]))
    put("hgn", _vec_cols(inputs["hg_norm_g"][0]))
    cvw = np.asarray(inputs["cv_w"], np.float32)[0, :, 0, :]
    put("cvw", np.ascontiguousarray(cvw.reshape(CVK, 4, P).transpose(2, 1, 0).reshape(P, 4 * CVK)))
    put("cvb", _vec_cols(inputs["cv_b"][0]))
    put("cvg", _vec_cols(inputs["cv_ln_g"][0]))
    put("cvbeta", _vec_cols(inputs["cv_ln_b"][0]))
    put("pscale", _vec_cols(inputs["pool_scale"][0]))
    put("lnmix_g", _vec_cols(inputs["ln_mix_g"]))
    put("lnmix_b", _vec_cols(inputs["ln_mix_b"]))
    put("lnxa_g", _vec_cols(inputs["ln_xa_g"]))
    put("lnxa_b", _vec_cols(inputs["ln_xa_b"]))
    put("lnffn_g", _vec_cols(inputs["ln_ffn_g"]))
    put("lnffn_b", _vec_cols(inputs["ln_ffn_b"]))
    return cfa


_NC_CACHE = {}


def run(inputs, NG, enable=("mix0", "xa0", "ffn0", "mix1", "xa1", "ffn1"), trace=False):
    inputs = {k: np.asarray(v) for k, v in inputs.items()}
    x = inputs["x"].astype(np.float32, copy=False)
    B, S, _ = x.shape
    half_len = NG * T
    assert S == 2 * half_len
    key = (NG, tuple(enable))
    if key not in _NC_CACHE:
        _NC_CACHE[key] = build(NG, enable)
    nc = _NC_CACHE[key]
    f32 = lambda a: np.ascontiguousarray(np.asarray(a, np.float32))
    shared = {
        "w_in": f32(inputs["ab_w_in"][0]),
        "w_out": f32(inputs["ab_w_out"][0]),
        "pool_w": f32(inputs["pool_w"][0].reshape(D, 256)),
    }
    for l in range(2):
        shared[f"wq{l}"] = f32(inputs["xa_wq"][l])
        shared[f"wk{l}"] = f32(inputs["xa_wk"][l])
        shared[f"wv{l}"] = f32(inputs["xa_wv"][l])
        shared[f"wo{l}"] = f32(inputs["xa_wo"][l])
        shared[f"wg{l}"] = f32(inputs["ffn_wg"][l])
        shared[f"wu{l}"] = f32(inputs["ffn_wu"][l])
        shared[f"wd{l}"] = f32(inputs["ffn_wd"][l])
    ncores = int(os.environ.get("MK_CORES", 2 * B))
    in_maps = []
    for i in range(ncores):
        b, half = i // 2, i % 2
        m = {k: v for k, v in shared.items() if k in nc._used_inputs}
        m["xcur"] = f32(x[b, half * half_len:(half + 1) * half_len])
        m["xprev"] = f32(x[b, 0:half_len]) if half == 1 else np.zeros((half_len, D), np.float32)
        m["mem"] = f32(inputs["mem"][b])
        m["cf"] = make_cf(inputs, half, half == 0)
        in_maps.append(m)
    res = run_bass_kernel_spmd(nc, in_maps, core_ids=list(range(ncores)), **({"trace": True} if trace else {}))
    out = np.empty((B, S, D), np.float32)
    for i in range(ncores):
        b, half = i // 2, i % 2
        out[b, half * half_len:(half + 1) * half_len] = res.results[i]["out"]
    return out, res


def kernel(**inputs):
    out, _ = run(inputs, NG_FULL)
    return out
```

```python
import os
import sys
import numpy as np
from contextlib import ExitStack
import concourse.bass as bass
import concourse.mybir as mybir
from concourse.bass_utils import run_bass_kernel_spmd

F32 = mybir.dt.float32
BF16 = mybir.dt.bfloat16
AF = mybir.ActivationFunctionType
ALU = mybir.AluOpType

P = 128
D = 1024
KC = 8
T = 512
DFF = 2816
NJ = 22
MEM = 256
ALPHA = 4.0 ** 0.25
LN_EPS = 1e-5
RMS_EPS = 1e-6
CVK = 31
HALO = 30
SLOT = 2048
NSLOT = 7
NSTG = 1
NIO = 3
NTF = 11
NG_FULL = 8


class Op:
    __slots__ = ("eng", "fn", "dma", "npieces", "deps", "stream", "sidx", "signal", "waits", "done", "line")

    def __init__(self, eng, fn, dma, npieces):
        self.eng = eng
        self.fn = fn
        self.dma = dma
        self.npieces = npieces
        self.signal = False


class Prog:
    def __init__(self):
        self.ops = []
        self.lastw = {}
        self.readers = {}

    def add(self, eng, fn, reads=(), writes=(), dma=None, npieces=1):
        op = Op(eng, fn, dma, npieces)
        op.stream = dma if dma is not None else eng
        f_ = sys._getframe(1)
        if f_.f_code.co_name == "A":
            f_ = f_.f_back
        op.line = f_.f_lineno
        deps = set()
        psr = [k for k in reads if isinstance(k, tuple) and k[0] == "ps"]
        if psr:
            reads = [k for k in reads if k not in psr]
            writes = list(writes) + psr
        for k in reads:
            w = self.lastw.get(k)
            if w is not None:
                deps.add(w)
        for k in writes:
            w = self.lastw.get(k)
            if w is not None:
                deps.add(w)
            rd = self.readers.get(k)
            if rd:
                deps.update(rd.values())
        for k in reads:
            self.readers.setdefault(k, {})[op.stream] = op
        for k in writes:
            self.lastw[k] = op
            self.readers[k] = {}
        deps.discard(op)
        op.deps = deps
        self.ops.append(op)
        return op

    def finalize(self):
        streams = {}
        for op in self.ops:
            lst = streams.setdefault(op.stream, [])
            op.sidx = len(lst)
            lst.append(op)
        clocks = {}
        for op in self.ops:
            clk = clocks.setdefault(op.eng, {})
            need = {}
            for d in op.deps:
                if d.stream == op.eng and op.eng == "pe":
                    continue
                if clk.get(d.stream, -1) < d.sidx:
                    if need.get(d.stream, -1) < d.sidx:
                        need[d.stream] = d.sidx
            op.waits = sorted(need.items())
            for s_, i_ in op.waits:
                dop = streams[s_][i_]
                dop.signal = True
                for k2, v2 in dop.done.items():
                    if clk.get(k2, -1) < v2:
                        clk[k2] = v2
            done = dict(clk)
            done[op.stream] = op.sidx
            op.done = done
            if op.dma is None and op.eng == "pe":
                clk["pe"] = op.sidx
        if os.environ.get("MK_ALLSIG"):
            for op in self.ops:
                if op.dma is None and op.eng != "sp":
                    op.signal = True
        self.val = {}
        for s_, lst in streams.items():
            cnt = 0
            vals = []
            for op in lst:
                if op.dma is not None:
                    cnt += 16 * op.npieces
                elif op.signal:
                    cnt += 1
                vals.append(cnt)
            self.val[s_] = vals
        self.streams = streams
        for op in self.ops:
            op.done = None

    def emit(self, nc, es):
        sems = {}
        for s_ in self.streams:
            sems[s_] = es.enter_context(nc.semaphore("sem_" + s_))
            if os.environ.get("MK_DUMP"):
                print("MKSEM", s_, sems[s_].num)
        blk = es.enter_context(nc.Block())
        eng_ops = {}
        for op in self.ops:
            eng_ops.setdefault(op.eng, []).append(op)

        def mk(eng):
            def body(e):
                for op in eng_ops.get(eng, []):
                    for s_, i_ in op.waits:
                        e.wait_ge(sems[s_], self.val[s_][i_])
                    if op.dma is not None:
                        op.fn(e, sems[op.dma])
                    else:
                        ins = op.fn(e)
                        if op.signal:
                            ins.then_inc(sems[eng], 1)
            return body

        blk.sync(mk("sp"))
        blk.tensor(mk("pe"))
        blk.scalar(mk("act"))
        blk.vector(mk("dve"))
        blk.gpsimd(mk("pool"))


def _cf_layout():
    cols = {}
    off = 0
    for name, n in [("ident", 128), ("scanmask", 512), ("cmask", 128), ("invcnt", 64), ("flag", 1),
                    ("lbp", 12), ("memg", 8), ("memb", 8), ("hgn", 4), ("cvw", 124), ("cvb", 4),
                    ("cvg", 4), ("cvbeta", 4), ("pscale", 8),
                    ("lnmix_g", 16), ("lnmix_b", 16), ("lnxa_g", 16), ("lnxa_b", 16),
                    ("lnffn_g", 16), ("lnffn_b", 16)]:
        cols[name] = (off, n)
        off += n
    return cols, off


CF_COLS, NCF = _cf_layout()


def _vec_cols(v):
    v = np.asarray(v, np.float32)
    if v.ndim == 1:
        v = v[None]
    L = v.shape[0]
    n = v.shape[1] // 128
    return np.ascontiguousarray(v.reshape(L, n, 128).transpose(2, 0, 1).reshape(128, L * n))


def build(NG, enable=("mix0", "xa0", "ffn0", "mix1", "xa1", "ffn1"), use_prefix=True):
    nc = bass.Bass("TRN2", target_bir_lowering=False)
    NTOK = NG * T
    dt = lambda name, shape, kind="ExternalInput", dtype=F32: nc.dram_tensor(name, shape, dtype, kind=kind).ap()
    xcur = dt("xcur", [NTOK, D])
    xprev = dt("xprev", [NTOK, D])
    memd = dt("mem", [MEM, D])
    cfd = dt("cf", [P, NCF])
    WSHAPE = {"win": ("w_in", [D, 3072]), "wout": ("w_out", [D, D]), "pw": ("pool_w", [D, 256])}
    for l in range(2):
        for nm_, shp in [("wq", [D, D]), ("wk", [D, D]), ("wv", [D, D]), ("wo", [D, D]), ("wg", [D, DFF]), ("wu", [D, DFF]), ("wd", [DFF, D])]:
            WSHAPE[f"{nm_}{l}"] = (f"{nm_}{l}", shp)

    class _W(dict):
        def __missing__(self, k):
            nm_, shp = WSHAPE[k]
            self[k] = dt(nm_, shp)
            return self[k]
    W = _W()
    outd = dt("out", [NTOK, D], kind="ExternalOutput")
    NPAN = 200 if len(enable) > 0 else 1
    scr = dt("wscr", [NPAN, P, SLOT], kind="Internal", dtype=BF16)

    pg = Prog()
    es = ExitStack()
    E = es.enter_context
    sb = lambda name, shape, dtype=F32: E(nc.sbuf_tensor("sb_" + name, shape, dtype))

    io = sb("io", [P, NIO, D])
    cf = sb("cf", [P, NCF])
    dv = sb("dv", [P, 128])
    xres = sb("xres", [P, 2, KC, T])
    xb = sb("xb", [P, 2, KC, T], BF16)
    TFL = sb("TFL", [P, 2, 2, T])
    G = sb("G", [P, 24, T], BF16)
    TF = sb("TF", [P, NTF, T])
    kendT = sb("kendT", [P, 4, T], BF16)
    vT = sb("vT", [P, 4, T], BF16)
    ubuf = sb("ubuf", [P, 4, HALO + T], BF16)
    diag = sb("diag", [P, CVK, P], BF16)
    S = sb("S", [P, 4, P])
    Sb = sb("Sb", [P, 8, 4, P], BF16)
    dcy = sb("dcy", [P, 4, 16])
    KT = sb("KT", [P, 2, KC, MEM], BF16)
    Vm = sb("Vm", [P, 2, 2, D], BF16)
    slots = sb("slots", [P, NSLOT, SLOT], BF16)
    ptmp = sb("ptmp", [P, 2, 16 + T])
    P8 = sb("P8", [P, KC, T], BF16)
    phalo = sb("phalo", [P, KC, 16])
    identb = sb("identb", [P, P], BF16)
    cmaskb = sb("cmaskb", [P, P], BF16)
    onesb = sb("onesb", [P, 4, P], BF16)
    ps = [E(nc.psum_tensor(f"ps{i}", [P, T], F32)) for i in range(8)]

    def cfc(name, a=0, n=None):
        o, w = CF_COLS[name]
        if n is None:
            n = w - a
        return cf[:, o + a:o + a + n]

    ident = cfc("ident")

    rr = {"A": 0, "B": 0, "C": 0, "L": 0}
    pools = {"A": [0, 1, 2], "L": [3, 4], "C": [5, 6], "B": [7]}

    dq = {"on": False, "q": [], "stream": None}

    def pump(n):
        if dq["on"]:
            return
        for _ in range(n):
            if not dq["q"]:
                return
            eng, fn, reads, writes, dma = dq["q"].pop(0)
            pg.add(eng, fn, reads, writes, dma=dma)

    def flush():
        pump(10 ** 9)

    def pbank(pool):
        pump(2)
        lst = pools[pool]
        b = lst[rr[pool] % len(lst)]
        rr[pool] += 1
        return b

    def A(eng, fn, reads=(), writes=(), dma=None):
        if dq["on"]:
            dq["q"].append((eng, fn, list(reads), list(writes), dma))
            return None
        return pg.add(eng, fn, reads, writes, dma=dma)

    def KX(s, c):
        return ("xres", s, c)

    def KB(s, c):
        return ("xb", s, c)

    def KG(i):
        return ("G", i)

    def KT_(i):
        return ("TF", i)

    def KP(b):
        return ("ps", b)

    wstate = {"slot": 0, "stg": 0, "gid": {}, "scrw": 0}

    def get_panel(wname, kc0, nk, col0, ncols, to_scratch=True):
        n = nk * ncols
        assert n <= SLOT
        key = (wname, kc0, nk, col0, ncols)
        sl = wstate["slot"] % NSLOT
        wstate["slot"] += 1
        dst = slots[:, sl, 0:n]
        view = dst.rearrange("p (k n) -> p k n", k=nk)
        if key in wstate["gid"]:
            gid = wstate["gid"][key]
            pg.add("sp", lambda e, sem, dst=dst, gid=gid, n=n: e.dma_start(out=dst, in_=scr[gid, :, 0:n]).then_inc(sem, 16),
                   reads=[("scr", gid)], writes=[("w", sl)], dma=f"w{sl}")
        else:
            gid = len(wstate["gid"])
            assert gid < NPAN
            wstate["gid"][key] = gid
            src = W[wname][kc0 * P:(kc0 + nk) * P, col0:col0 + ncols].rearrange("(k p) n -> p k n", p=P)
            pg.add("pool", lambda e, sem, view=view, src=src: e.dma_start(out=view, in_=src).then_inc(sem, 16),
                   writes=[("w", sl)], dma=f"wl{sl}")
            if to_scratch:
                pg.add("sp", lambda e, sem, dst=dst, gid=gid, n=n: e.dma_start(out=scr[gid, :, 0:n], in_=dst).then_inc(sem, 16),
                       reads=[("w", sl)], writes=[("scr", gid)], dma=f"sw{sl}")
        return sl, view

    def proj_fm(wname, nk, col_chunks, act_ap, act_keys, evac, N=T, kc_base=0, cols_per_panel=256):
        cpp = cols_per_panel // P
        i = 0
        while i < len(col_chunks):
            grp = [col_chunks[i]]
            while len(grp) < cpp and i + len(grp) < len(col_chunks) and col_chunks[i + len(grp)] == grp[-1] + 1:
                grp.append(col_chunks[i + len(grp)])
            sl, view = get_panel(wname, kc_base, nk, grp[0] * P, len(grp) * P)
            for gi_, oc in enumerate(grp):
                b = pbank("A")

                def fn(e, b=b, view=view, gi_=gi_):
                    ins = None
                    for kc in range(nk):
                        ins = e.matmul(ps[b][:, 0:N], lhsT=view[:, kc, gi_ * P:(gi_ + 1) * P], rhs=act_ap(kc),
                                       start=(kc == 0), stop=(kc == nk - 1))
                    return ins
                A("pe", fn, reads=[("w", sl)] + list(act_keys), writes=[KP(b)])
                evac(i + gi_, oc, b)
            i += len(grp)

    pg.add("sp", lambda e, sem: e.dma_start(out=cf[:], in_=cfd).then_inc(sem, 16), writes=["cf"], dma="cfl")
    A("pool", lambda e: e.memset(onesb[:, 0, :], 1.0 / 1024), writes=["ones0"])
    A("pool", lambda e: e.memset(onesb[:, 1, :], 1.0 / 512), writes=["ones1"])
    A("pool", lambda e: e.memset(onesb[:, 2, :], 1.0 / 128), writes=["ones2"])
    A("pool", lambda e: e.memset(onesb[:, 3, :], 1.0), writes=["ones3"])
    A("pool", lambda e: e.memset(S[:], 0.0), writes=[("S", h) for h in range(4)])
    A("pool", lambda e: e.memset(ubuf[:], 0.0), writes=[("ubuf", c) for c in range(4)])
    A("pool", lambda e: e.memset(phalo[:], 0.0), writes=[("phalo", c) for c in range(KC)])
    A("pool", lambda e: e.memset(ptmp[:], 0.0), writes=[("ptmp", i) for i in range(2)])
    A("dve", lambda e: e.tensor_copy(out=identb[:], in_=ident), reads=["cf"], writes=["identb"])
    A("dve", lambda e: e.tensor_copy(out=cmaskb[:], in_=cfc("cmask")), reads=["cf"], writes=["cmaskb"])
    A("act", lambda e: e.activation(out=dv[:, 20:32], in_=cfc("lbp"), func=AF.Exp), reads=["cf"], writes=["dvt"])
    A("dve", lambda e: e.tensor_tensor(out=dv[:, 0:4], in0=dv[:, 20:24], in1=dv[:, 24:28], op=ALU.add), reads=["dvt"], writes=["dv0"])
    A("dve", lambda e: e.tensor_tensor(out=dv[:, 0:4], in0=dv[:, 0:4], in1=dv[:, 28:32], op=ALU.add), reads=["dvt", "dv0"], writes=["dv0"])
    A("dve", lambda e: e.reciprocal(out=dv[:, 4:8], in_=dv[:, 0:4]), reads=["dv0"], writes=["dv1"])
    A("dve", lambda e: e.tensor_tensor(out=dv[:, 0:4], in0=dv[:, 20:24], in1=dv[:, 4:8], op=ALU.mult), reads=["dvt", "dv1", "dv0"], writes=["dv0"])
    A("dve", lambda e: e.tensor_scalar(out=dv[:, 4:8], in0=dv[:, 0:4], scalar1=-1.0, scalar2=1.0, op0=ALU.mult, op1=ALU.add), reads=["dv0", "dv1"], writes=["dv1"])
    A("dve", lambda e: e.tensor_scalar(out=dv[:, 8:12], in0=dv[:, 0:4], scalar1=-1.0, scalar2=None, op0=ALU.add), reads=["dv0"], writes=["dv2"])
    A("dve", lambda e: e.tensor_scalar(out=dv[:, 12:20], in0=cfc("pscale"), scalar1=1.0 / ALPHA, scalar2=None, op0=ALU.mult), reads=["cf"], writes=["dv3"])
    LNN = ["lnmix", "lnxa", "lnffn"]
    for i_, nm in enumerate(LNN):
        for l in range(2):
            o_ = 32 + 16 * (i_ * 2 + l)
            A("dve", lambda e, o_=o_, nm=nm, l=l: e.tensor_scalar(out=dv[:, o_:o_ + 8], in0=cfc(nm + "_g", 8 * l, 8), scalar1=ALPHA, scalar2=None, op0=ALU.mult), reads=["cf"], writes=[("dvln", o_)])
            A("dve", lambda e, o_=o_, nm=nm, l=l: e.tensor_scalar(out=dv[:, o_ + 8:o_ + 16], in0=cfc(nm + "_b", 8 * l, 8), scalar1=ALPHA, scalar2=None, op0=ALU.mult), reads=["cf"], writes=[("dvln", o_ + 8)])
    DVK = ["dv0", "dv1", "dv2", "dv3"] + [("dvln", 32 + 8 * i) for i in range(12)]

    def ln_cols(nm, l, scaled):
        if scaled:
            o_ = 32 + 16 * (LNN.index(nm) * 2 + l)
            return (lambda c: dv[:, o_ + c:o_ + c + 1]), (lambda c: dv[:, o_ + 8 + c:o_ + 9 + c])
        return (lambda c: cfc(nm + "_g", 8 * l + c, 1)), (lambda c: cfc(nm + "_b", 8 * l + c, 1))

    def ln_stats(nch, N, zc, zkeys, ones_idx, eps, s, zb_done=False, pool="L", defer=False):
        bs, bq = pbank(pool), pbank(pool)
        K12, K13 = ("TFL", s, 0), ("TFL", s, 1)
        for c in range(nch):
            if not zb_done:
                A("dve", lambda e, c=c: e.tensor_copy(out=G[:, c, 0:N], in_=zc(c)), reads=[zkeys[c]], writes=[KG(c)])
            A("act", lambda e, c=c: e.activation(out=G[:, 8 + c, 0:N], in_=zc(c), func=AF.Square), reads=[zkeys[c]], writes=[KG(8 + c)])
            A("pe", lambda e, c=c: e.matmul(ps[bs][:, 0:N], lhsT=onesb[:, ones_idx, :], rhs=G[:, c, 0:N], start=(c == 0), stop=(c == nch - 1)),
              reads=[KG(c), f"ones{ones_idx}"], writes=[KP(bs)])
            A("pe", lambda e, c=c: e.matmul(ps[bq][:, 0:N], lhsT=onesb[:, ones_idx, :], rhs=G[:, 8 + c, 0:N], start=(c == 0), stop=(c == nch - 1)),
              reads=[KG(8 + c), f"ones{ones_idx}"], writes=[KP(bq)])
        t0 = TFL[:, s, 0, 0:N]
        t1 = TFL[:, s, 1, 0:N]
        if defer:
            dq["on"] = True
            dq["stream"] = s
        A("act", lambda e: e.activation(out=t0, in_=ps[bs][:, 0:N], func=AF.Square), reads=[KP(bs)], writes=[K12])
        A("dve", lambda e: e.scalar_tensor_tensor(out=t0, in0=t0, scalar=-1.0, in1=ps[bq][:, 0:N], op0=ALU.mult, op1=ALU.add), reads=[K12, KP(bq)], writes=[K12])
        A("dve", lambda e: e.tensor_scalar(out=t0, in0=t0, scalar1=float(eps), scalar2=None, op0=ALU.add), reads=[K12], writes=[K12])
        A("act", lambda e: e.activation(out=t1, in_=t0, func=AF.Ln), reads=[K12], writes=[K13])
        A("act", lambda e: e.activation(out=t1, in_=t1, func=AF.Exp, scale=-0.5), reads=[K13], writes=[K13])
        A("dve", lambda e: e.scalar_tensor_tensor(out=t0, in0=ps[bs][:, 0:N], scalar=-1.0, in1=t1, op0=ALU.mult, op1=ALU.mult), reads=[KP(bs), K13, K12], writes=[K12])
        return t1, t0

    def ln_norm(nch, N, zc, zkeys, rstd, nmr, s):
        K12, K13 = ("TFL", s, 0), ("TFL", s, 1)
        for c in range(nch):
            A("dve", lambda e, c=c: e.tensor_tensor(out=zc(c), in0=zc(c), in1=rstd, op=ALU.mult), reads=[zkeys[c], K13], writes=[zkeys[c]])
            A("dve", lambda e, c=c: e.tensor_tensor(out=zc(c), in0=zc(c), in1=nmr, op=ALU.add), reads=[zkeys[c], K12], writes=[zkeys[c]])

    def ln_stream(nm, l, s, final=False):
        zc = lambda c: xres[:, s, c, :]
        zk = [KX(s, c) for c in range(KC)]
        flush()
        rstd, nmr = ln_stats(KC, T, zc, zk, 0, LN_EPS, s, zb_done=True, defer=True)
        ln_norm(KC, T, zc, zk, rstd, nmr, s)
        g1, b1 = ln_cols(nm, l, False)
        g2, b2 = ln_cols(nm, l, not final)
        for c in range(KC):
            A("act", lambda e, c=c: e.activation(out=xb[:, s, c, :], in_=xres[:, s, c, :], func=AF.Identity, scale=g1(c), bias=b1(c)),
              reads=[KX(s, c), "cf"], writes=[KB(s, c)])
            A("act", lambda e, c=c: e.activation(out=xres[:, s, c, :], in_=xres[:, s, c, :], func=AF.Identity, scale=g2(c), bias=b2(c)),
              reads=[KX(s, c), "cf"] + DVK, writes=[KX(s, c)])
        if final and final > 0:
            store_out(final - 1, s)
        dq["on"] = False

    def evac_resid(s, scale=None):
        def ev(i, oc, b):
            if scale is None:
                A("dve", lambda e: e.tensor_tensor(out=xres[:, s, oc, :], in0=ps[b][:], in1=xres[:, s, oc, :], op=ALU.add), reads=[KP(b), KX(s, oc)], writes=[KX(s, oc)])
            else:
                A("dve", lambda e: e.scalar_tensor_tensor(out=xres[:, s, oc, :], in0=ps[b][:], scalar=scale(oc), in1=xres[:, s, oc, :], op0=ALU.mult, op1=ALU.add),
                  reads=[KP(b), KX(s, oc)] + DVK, writes=[KX(s, oc)])
        return ev

    def zb_cast(s):
        for c in range(KC):
            A("dve", lambda e, c=c: e.tensor_copy(out=G[:, c, :], in_=xres[:, s, c, :]), reads=[KX(s, c)], writes=[KG(c)])

    iost = {"n": 0}

    def load_x(src, row0, need_res, s):
        for tt in range(4):
            r = iost["n"] % 2
            iost["n"] += 1
            pg.add("pool", lambda e, sem, r=r, tt=tt: e.dma_start(out=io[:, r, :], in_=src[row0 + tt * P:row0 + (tt + 1) * P, :]).then_inc(sem, 16),
                   writes=[("io", r)], dma=f"io{r}")
            for half in range(2):
                b = pbank("C")

                def fn(e, b=b, r=r, half=half):
                    ins = None
                    for j in range(4):
                        c = half * 4 + j
                        ins = e.transpose(ps[b][:, j * P:(j + 1) * P], io[:, r, c * P:(c + 1) * P], ident)
                    return ins
                A("pe", fn, reads=[("io", r), "cf"], writes=[KP(b)])
                src_v = ps[b][:].rearrange("p (j t) -> p j t", j=4)
                sl_ = slice(tt * P, (tt + 1) * P)
                cs = slice(half * 4, half * 4 + 4)
                A("dve", lambda e, src_v=src_v, cs=cs, sl_=sl_: e.tensor_copy(out=xb[:, s, cs, sl_], in_=src_v),
                  reads=[KP(b)], writes=[KB(s, c) for c in range(half * 4, half * 4 + 4)])
                if need_res:
                    A("act", lambda e, src_v=src_v, cs=cs, sl_=sl_: e.mul(out=xres[:, s, cs, sl_], in_=src_v, mul=float(ALPHA)),
                      reads=[KP(b)], writes=[KX(s, c) for c in range(half * 4, half * 4 + 4)])

    out_ops = []

    def store_out(row0, s):
        for tt in range(4):
            r = 2
            for half in range(2):
                b = pbank("L")

                def fn(e, b=b, half=half, tt=tt):
                    ins = None
                    for j in range(4):
                        c = half * 4 + j
                        ins = e.transpose(ps[b][:, j * P:(j + 1) * P], xres[:, s, c, tt * P:(tt + 1) * P], ident)
                    return ins
                A("pe", fn, reads=[KX(s, c) for c in range(half * 4, half * 4 + 4)] + ["cf"], writes=[KP(b)])
                eng = "act" if half == 0 else "dve"
                if eng == "act":
                    A("act", lambda e, b=b, r=r, half=half: e.copy(out=io[:, r, half * T:(half + 1) * T], in_=ps[b][:]), reads=[KP(b)], writes=[("io", r)])
                else:
                    A("dve", lambda e, b=b, r=r, half=half: e.tensor_copy(out=io[:, r, half * T:(half + 1) * T], in_=ps[b][:]), reads=[KP(b)], writes=[("io", r)])
            A("pool", lambda e, sem, r=r, tt=tt: e.dma_start(out=outd[row0 + tt * P:row0 + (tt + 1) * P, :], in_=io[:, r, :]).then_inc(sem, 16),
              reads=[("io", r)], writes=[("out", row0, tt)], dma=f"io{r}")
            out_ops.append(("out", row0, tt))

    def ffn(l, s, store_row0=None):
        xk = [KB(s, c) for c in range(KC)]
        for j0 in range(0, NJ, 2):
            slg, vg = get_panel(f"wg{l}", 0, KC, j0 * P, 2 * P)
            slu, vu = get_panel(f"wu{l}", 0, KC, j0 * P, 2 * P)
            for jj in range(2):
                j = j0 + jj
                bg, bu = pbank("A"), pbank("A")

                def fng(e, bg=bg, vg=vg, jj=jj):
                    ins = None
                    for kc in range(KC):
                        ins = e.matmul(ps[bg][:], lhsT=vg[:, kc, jj * P:(jj + 1) * P], rhs=xb[:, s, kc, :], start=(kc == 0), stop=(kc == KC - 1))
                    return ins

                def fnu(e, bu=bu, vu=vu, jj=jj):
                    ins = None
                    for kc in range(KC):
                        ins = e.matmul(ps[bu][:], lhsT=vu[:, kc, jj * P:(jj + 1) * P], rhs=xb[:, s, kc, :], start=(kc == 0), stop=(kc == KC - 1))
                    return ins
                A("pe", fng, reads=[("w", slg)] + xk, writes=[KP(bg)])
                A("pe", fnu, reads=[("w", slu)] + xk, writes=[KP(bu)])
                tix = (10, 0)[j % 2]
                A("act", lambda e, bg=bg, tix=tix: e.activation(out=TF[:, tix, :], in_=ps[bg][:], func=AF.Silu), reads=[KP(bg)], writes=[KT_(tix)])
                A("dve", lambda e, bu=bu, tix=tix, j=j: e.tensor_tensor(out=G[:, j, :], in0=ps[bu][:], in1=TF[:, tix, :], op=ALU.mult),
                  reads=[KP(bu), KT_(tix)], writes=[KG(j)])
        ev = evac_resid(s)
        for c in range(KC):
            b = pbank("A")
            for hf in range(2):
                sl, view = get_panel(f"wd{l}", hf * 11, 11, c * P, P)

                def fn(e, b=b, view=view, hf=hf):
                    ins = None
                    for k in range(11):
                        j = hf * 11 + k
                        ins = e.matmul(ps[b][:], lhsT=view[:, k, :], rhs=G[:, j, :], start=(j == 0), stop=(j == NJ - 1))
                    return ins
                A("pe", fn, reads=[("w", sl)] + [KG(hf * 11 + k) for k in range(11)], writes=[KP(b)])
            ev(c, c, b)
        zb_cast(s)
        ln_stream("lnffn", l, s, final=(0 if l == 0 else (store_row0 + 1 if store_row0 is not None else -1)))

    def xattn(l, s):
        xk = [KB(s, c) for c in range(KC)]

        def evq(i, oc, b):
            if oc % 2 == 0:
                A("act", lambda e: e.mul(out=G[:, oc, :], in_=ps[b][:], mul=1.0 / 16.0), reads=[KP(b)], writes=[KG(oc)])
            else:
                A("dve", lambda e: e.tensor_scalar(out=G[:, oc, :], in0=ps[b][:], scalar1=1.0 / 16.0, scalar2=None, op0=ALU.mult), reads=[KP(b)], writes=[KG(oc)])
        proj_fm(f"wq{l}", KC, list(range(KC)), lambda kc: xb[:, s, kc, :], xk, evq)
        for h in range(4):
            pk = [16 + (h % 2) * 2, 17 + (h % 2) * 2]
            for mc in range(2):
                b = pbank("C")

                def fn(e, b=b, h=h, mc=mc):
                    ins = None
                    for dc in range(2):
                        ins = e.matmul(ps[b][:], lhsT=KT[:, l, 2 * h + dc, mc * P:(mc + 1) * P], rhs=G[:, 2 * h + dc, :], start=(dc == 0), stop=(dc == 1))
                    return ins
                A("pe", fn, reads=[("KT", l), KG(2 * h), KG(2 * h + 1)], writes=[KP(b)])
                A("act", lambda e, b=b, g_=pk[mc]: e.activation(out=G[:, g_, :], in_=ps[b][:], func=AF.Exp), reads=[KP(b)], writes=[KG(pk[mc])])
            bsum = pbank("B")

            def fns(e, bsum=bsum, pk=pk):
                ins = None
                for mc in range(2):
                    ins = e.matmul(ps[bsum][:], lhsT=onesb[:, 3, :], rhs=G[:, pk[mc], :], start=(mc == 0), stop=(mc == 1))
                return ins
            A("pe", fns, reads=[KG(pk[0]), KG(pk[1]), "ones3"], writes=[KP(bsum)])
            tix = (10, 0)[h % 2]
            A("act", lambda e, bsum=bsum, tix=tix: e.activation(out=TF[:, tix, :], in_=ps[bsum][:], func=AF.Ln), reads=[KP(bsum)], writes=[KT_(tix)])
            A("act", lambda e, tix=tix: e.activation(out=TF[:, tix, :], in_=TF[:, tix, :], func=AF.Exp, scale=-1.0), reads=[KT_(tix)], writes=[KT_(tix)])
            for dc in range(2):
                b = pbank("A")

                def fno(e, b=b, h=h, dc=dc, pk=pk):
                    ins = None
                    for mc in range(2):
                        ins = e.matmul(ps[b][:], lhsT=Vm[:, l, mc, (2 * h + dc) * P:(2 * h + dc + 1) * P], rhs=G[:, pk[mc], :], start=(mc == 0), stop=(mc == 1))
                    return ins
                A("pe", fno, reads=[("Vm", l), KG(pk[0]), KG(pk[1])], writes=[KP(b)])
                A("dve", lambda e, b=b, tix=tix, oc=2 * h + dc: e.tensor_tensor(out=G[:, 8 + oc, :], in0=ps[b][:], in1=TF[:, tix, :], op=ALU.mult),
                  reads=[KP(b), KT_(tix)], writes=[KG(8 + 2 * h + dc)])
        proj_fm(f"wo{l}", KC, list(range(KC)), lambda kc: G[:, 8 + kc, :], [KG(8 + c) for c in range(KC)], evac_resid(s))
        zb_cast(s)
        ln_stream("lnxa", l, s)

    def mem_kv():
        memT = lambda c: TF[:, c // 2, (c % 2) * MEM:(c % 2 + 1) * MEM]
        mk = [("memT", c) for c in range(KC)]
        for mt in range(2):
            r = iost["n"] % 2
            iost["n"] += 1
            pg.add("pool", lambda e, sem, r=r, mt=mt: e.dma_start(out=io[:, r, :], in_=memd[mt * P:(mt + 1) * P, :]).then_inc(sem, 16), writes=[("io", r)], dma=f"io{r}")
            for half in range(2):
                b = pbank("C")

                def fn(e, b=b, r=r, half=half):
                    ins = None
                    for j in range(4):
                        c = half * 4 + j
                        ins = e.transpose(ps[b][:, j * P:(j + 1) * P], io[:, r, c * P:(c + 1) * P], ident)
                    return ins
                A("pe", fn, reads=[("io", r), "cf"], writes=[KP(b)])
                for j in range(4):
                    c = half * 4 + j
                    A("dve", lambda e, b=b, j=j, c=c, mt=mt: e.tensor_copy(out=memT(c)[:, mt * P:(mt + 1) * P], in_=ps[b][:, j * P:(j + 1) * P]),
                      reads=[KP(b)], writes=[mk[c], KT_(c // 2)])
        rstd, nmr = ln_stats(KC, MEM, memT, mk, 0, LN_EPS, 0)
        ln_norm(KC, MEM, memT, mk, rstd, nmr, 0)
        mn = lambda c: G[:, 16 + c // 2, (c % 2) * MEM:(c % 2 + 1) * MEM]
        mnk = [KG(16 + c // 2) for c in range(KC)]
        for c in range(KC):
            A("dve", lambda e, c=c: e.tensor_scalar(out=mn(c), in0=memT(c), scalar1=cfc("memg", c, 1), scalar2=cfc("memb", c, 1), op0=ALU.mult, op1=ALU.add),
              reads=[mk[c], "cf"], writes=[mnk[c]])
        for l in range(2):
            def evk(i, oc, b, l=l):
                A("act", lambda e: e.copy(out=KT[:, l, oc, :], in_=ps[b][:, 0:MEM]), reads=[KP(b)], writes=[("KT", l)])
            proj_fm(f"wk{l}", KC, list(range(KC)), mn, list(set(mnk)), evk, N=MEM)
            for cp in range(4):
                sl, view = get_panel(f"wv{l}", 0, KC, cp * 256, 256, to_scratch=False)
                for mc in range(2):
                    b = pbank("A")

                    def fn(e, b=b, view=view, mc=mc):
                        ins = None
                        for kc in range(KC):
                            ins = e.matmul(ps[b][:, 0:256], lhsT=mn(kc)[:, mc * P:(mc + 1) * P], rhs=view[:, kc, :], start=(kc == 0), stop=(kc == KC - 1))
                        return ins
                    A("pe", fn, reads=[("w", sl)] + list(set(mnk)), writes=[KP(b)])
                    A("act", lambda e, b=b, mc=mc, cp=cp, l=l: e.copy(out=Vm[:, l, mc, cp * 256:(cp + 1) * 256], in_=ps[b][:, 0:256]), reads=[KP(b)], writes=[("Vm", l)])

    def mix1_chain(first_group, s):
        dq["on"] = True
        dq["stream"] = s
        for c in range(KC):
            gi = c // 2
            w = 2 << gi
            nsteps = gi + 1
            pb0 = 0
            bufs = [("ptmp", pb0), ("ptmp", pb0 + 1)]
            A("dve", lambda e, c=c, pb0=pb0: e.tensor_copy(out=ptmp[:, pb0, 0:16], in_=phalo[:, c, :]), reads=[("phalo", c)], writes=[bufs[0]])
            A("act", lambda e, c=c, pb0=pb0: e.copy(out=ptmp[:, pb0, 16:16 + T], in_=xres[:, s, c, :]), reads=[KX(s, c)], writes=[bufs[0]])
            A("dve", lambda e, c=c: e.tensor_copy(out=phalo[:, c, :], in_=xres[:, s, c, T - 16:T]), reads=[KX(s, c), bufs[0]], writes=[("phalo", c)])
            cur = 0
            sh = 1
            for s_ in range(nsteps):
                nxt = 1 - cur
                A("dve", lambda e, cur=cur, nxt=nxt, sh=sh, pb0=pb0: e.tensor_tensor(out=ptmp[:, pb0 + nxt, sh:16 + T], in0=ptmp[:, pb0 + cur, sh:16 + T], in1=ptmp[:, pb0 + cur, 0:16 + T - sh], op=ALU.add),
                  reads=[bufs[cur]], writes=[bufs[nxt]])
                cur = nxt
                sh *= 2
            A("dve", lambda e, cur=cur, c=c, w=w, pb0=pb0: e.scalar_tensor_tensor(out=P8[:, c, :], in0=ptmp[:, pb0 + cur, 16:16 + T], scalar=1.0 / w, in1=xres[:, s, c, :], op0=ALU.mult, op1=ALU.subtract),
              reads=[bufs[cur], KX(s, c)], writes=[("P8", c)])
            if first_group:
                A("dve", lambda e, cur=cur, gi=gi, pb0=pb0: e.tensor_tensor(out=ptmp[:, pb0 + cur, 16:32], in0=ptmp[:, pb0 + cur, 16:32], in1=cfc("invcnt", gi * 16, 16), op=ALU.mult),
                  reads=[bufs[cur], "cf"], writes=[bufs[cur]])
                A("dve", lambda e, cur=cur, c=c, pb0=pb0: e.tensor_tensor(out=P8[:, c, 0:16], in0=ptmp[:, pb0 + cur, 16:32], in1=xres[:, s, c, 0:16], op=ALU.subtract),
                  reads=[bufs[cur], KX(s, c)], writes=[("P8", c)])
        dq["on"] = False

    def mix1(first_group, s):
        ev = evac_resid(s, scale=lambda oc: dv[:, 12 + oc:13 + oc])
        for gi in range(4):
            sl, view = get_panel("pw", 2 * gi, 2, 0, 256)
            for oo in range(2):
                oc = 2 * gi + oo
                b = pbank("A")

                def fn(e, b=b, view=view, gi=gi, oo=oo):
                    ins = None
                    for kc in range(2):
                        ins = e.matmul(ps[b][:], lhsT=view[:, kc, oo * P:(oo + 1) * P], rhs=P8[:, 2 * gi + kc, :], start=(kc == 0), stop=(kc == 1))
                    return ins
                A("pe", fn, reads=[("w", sl), ("P8", 2 * gi), ("P8", 2 * gi + 1)], writes=[KP(b)])
                ev(oc, oc, b)
        zb_cast(s)
        ln_stream("lnmix", 1, s)

    sbst = {"n": 0}

    def mix0(need_out, need_u, s):
        xk = [KB(s, c) for c in range(KC)]
        act = lambda kc: xb[:, s, kc, :]
        for cp in range(2):
            sl, view = get_panel("win", 0, KC, 1024 + cp * 256, 256)
            for tt in range(4):
                b = pbank("A")

                def fn(e, b=b, view=view, tt=tt):
                    ins = None
                    for kc in range(KC):
                        ins = e.matmul(ps[b][:, 0:256], lhsT=xb[:, s, kc, tt * P:(tt + 1) * P], rhs=view[:, kc, :], start=(kc == 0), stop=(kc == KC - 1))
                    return ins
                A("pe", fn, reads=[("w", sl)] + xk, writes=[KP(b)])
                A("act", lambda e, b=b, tt=tt, cp=cp: e.copy(out=vT[:, tt, cp * 256:(cp + 1) * 256], in_=ps[b][:, 0:256]), reads=[KP(b)], writes=[("vT", tt)])
        def evf(i, h, b):
            sig, logf, kk, bb, eb, enb, blb = [TF[:, t_, :] for t_ in range(7)]
            A("act", lambda e: e.activation(out=sig, in_=ps[b][:], func=AF.Exp, scale=-1.0), reads=[KP(b)], writes=[KT_(0)])
            A("act", lambda e: e.activation(out=logf, in_=sig, func=AF.Ln, scale=dv[:, h:h + 1], bias=1.0), reads=[KT_(0)] + DVK, writes=[KT_(1)])
            A("act", lambda e: e.activation(out=enb, in_=sig, func=AF.Ln, bias=1.0), reads=[KT_(0)], writes=[KT_(5)])
            A("dve", lambda e: e.tensor_tensor(out=logf, in0=logf, in1=enb, op=ALU.subtract), reads=[KT_(1), KT_(5)], writes=[KT_(1)])
            A("act", lambda e: e.activation(out=enb, in_=enb, func=AF.Exp, scale=-1.0), reads=[KT_(5)], writes=[KT_(5)])
            A("dve", lambda e: e.scalar_tensor_tensor(out=kk, in0=sig, scalar=dv[:, 4 + h:5 + h], in1=enb, op0=ALU.mult, op1=ALU.mult), reads=[KT_(0), KT_(5)] + DVK, writes=[KT_(2)])
            A("dve", lambda e: e.tensor_tensor_scan(out=bb, data0=cfc("scanmask"), data1=logf, initial=0.0, op0=ALU.mult, op1=ALU.add), reads=[KT_(1), "cf"], writes=[KT_(3)])
            bv = bb.rearrange("p (n c) -> p n c", c=32)
            A("dve", lambda e: e.tensor_tensor(out=blb.rearrange("p (n c) -> p n c", c=32), in0=bv[:, :, 31:32].to_broadcast([P, 16, 32]), in1=bv, op=ALU.subtract),
              reads=[KT_(3)], writes=[KT_(6)])
            A("act", lambda e: e.activation(out=eb, in_=bb, func=AF.Exp), reads=[KT_(3)], writes=[KT_(4)])
            A("act", lambda e: e.activation(out=blb, in_=blb, func=AF.Exp), reads=[KT_(6)], writes=[KT_(6)])
            A("dve", lambda e: e.tensor_copy(out=dcy[:, h, :], in_=eb.rearrange("p (n c) -> p n c", c=32)[:, :, 31]), reads=[KT_(4)], writes=[("dcy", h)])
            A("dve", lambda e: e.tensor_tensor(out=G[:, 8 + h, :], in0=kk, in1=blb, op=ALU.mult), reads=[KT_(2), KT_(6)], writes=[KG(8 + h)])
            if need_out:
                A("act", lambda e: e.activation(out=enb, in_=bb, func=AF.Exp, scale=-1.0), reads=[KT_(3)], writes=[KT_(5)])
                A("dve", lambda e: e.tensor_tensor(out=G[:, 4 + h, :], in0=kk, in1=enb, op=ALU.mult), reads=[KT_(2), KT_(5)], writes=[KG(4 + h)])
                def evq(i2, oc2, b2):
                    A("dve", lambda e: e.tensor_tensor(out=G[:, h, :], in0=ps[b2][:], in1=eb, op=ALU.mult), reads=[KP(b2), KT_(4)], writes=[KG(h)])
                if h % 2 == 0:
                    qpan["p"] = get_panel("win", 0, KC, h * P, 2 * P)
                slq, vq = qpan["p"]
                bq_ = pbank("A")

                def fnq(e, bq_=bq_, vq=vq, gi_=h % 2):
                    ins = None
                    for kc in range(KC):
                        ins = e.matmul(ps[bq_][:], lhsT=vq[:, kc, gi_ * P:(gi_ + 1) * P], rhs=xb[:, s, kc, :], start=(kc == 0), stop=(kc == KC - 1))
                    return ins
                A("pe", fnq, reads=[("w", slq)] + xk, writes=[KP(bq_)])
                evq(0, h, bq_)
        qpan = {}
        proj_fm("win", KC, [4 + h for h in range(4)], act, xk, lambda i, oc, b: evf(i, oc - 4, b), cols_per_panel=256)
        pieces = []
        if need_out:
            def evg(i, oc, b):
                h = oc - 12
                A("act", lambda e: e.activation(out=TF[:, 7, :], in_=ps[b][:], func=AF.Silu), reads=[KP(b)], writes=[KT_(7)])
                A("dve", lambda e: e.tensor_scalar(out=G[:, 12 + h, :], in0=TF[:, 7, :], scalar1=cfc("hgn", h, 1), scalar2=None, op0=ALU.mult), reads=[KT_(7), "cf"], writes=[KG(12 + h)])
            pieces.append(lambda: proj_fm("win", KC, [12, 13], act, xk, evg))
            pieces.append(lambda: proj_fm("win", KC, [14, 15], act, xk, evg))
        def cpiece(cc):
            bga = pbank("A")
            sla, va = get_panel("win", 0, KC, (16 + cc) * P, P)
            slg, vg = get_panel("win", 0, KC, (20 + cc) * P, P)
            bgg = pbank("A")

            def fna(e, b=bga, v=va):
                ins = None
                for kc in range(KC):
                    ins = e.matmul(ps[b][:], lhsT=v[:, kc, :], rhs=xb[:, s, kc, :], start=(kc == 0), stop=(kc == KC - 1))
                return ins

            def fngt(e, b=bgg, v=vg):
                ins = None
                for kc in range(KC):
                    ins = e.matmul(ps[b][:], lhsT=v[:, kc, :], rhs=xb[:, s, kc, :], start=(kc == 0), stop=(kc == KC - 1))
                return ins
            A("pe", fna, reads=[("w", sla)] + xk, writes=[KP(bga)])
            A("pe", fngt, reads=[("w", slg)] + xk, writes=[KP(bgg)])
            A("act", lambda e, b=bgg: e.activation(out=TF[:, 7, :], in_=ps[b][:], func=AF.Sigmoid), reads=[KP(bgg)], writes=[KT_(7)])
            A("dve", lambda e, b=bga, cc=cc: e.tensor_tensor(out=ubuf[:, cc, HALO:HALO + T], in0=ps[b][:], in1=TF[:, 7, :], op=ALU.mult), reads=[KP(bga), KT_(7)], writes=[("ubuf", cc)])
            if need_out:
                A("pool", lambda e, cc=cc: e.tensor_tensor(out=diag[:], in0=identb[:].unsqueeze(1).to_broadcast([P, CVK, P]),
                                                           in1=cfc("cvw", cc * CVK, CVK).unsqueeze(2).to_broadcast([P, CVK, P]), op=ALU.mult),
                  reads=["identb", "cf"], writes=["diag"])
                bc = pbank("A")

                def fnc(e, bc=bc, cc=cc):
                    ins = None
                    for j in range(CVK):
                        ins = e.matmul(ps[bc][:], lhsT=diag[:, j, :], rhs=ubuf[:, cc, j:j + T], start=(j == 0), stop=(j == CVK - 1))
                    return ins
                A("pe", fnc, reads=["diag", ("ubuf", cc)], writes=[KP(bc)])
                A("act", lambda e, bc=bc, cc=cc: e.activation(out=TF[:, cc, :], in_=ps[bc][:], func=AF.Identity, bias=cfc("cvb", cc, 1)), reads=[KP(bc), "cf"], writes=[KT_(cc)])
            A("dve", lambda e, cc=cc: e.tensor_copy(out=ubuf[:, cc, 0:HALO], in_=ubuf[:, cc, T:T + HALO]), reads=[("ubuf", cc)], writes=[("ubuf", cc)])
        if need_u or need_out:
            for cc in range(4):
                pieces.append(lambda cc=cc: cpiece(cc))
        def chain(tt):
            ts_ = slice(tt * P, (tt + 1) * P)
            bt = pbank("A")
            ptv = ps[bt][:].bitcast(BF16)

            def fnt(e, ptv=ptv, ts_=ts_):
                ins = None
                for h in range(4):
                    ins = e.transpose(ptv[:, h * P:(h + 1) * P], G[:, 8 + h, ts_], identb[:])
                return ins
            A("pe", fnt, reads=[KG(8 + h) for h in range(4)] + ["identb"], writes=[KP(bt)])
            A("act", lambda e, ptv=ptv, tt=tt: e.copy(out=kendT[:, tt, :], in_=ptv[:, 0:T]), reads=[KP(bt)], writes=[("kendT", tt)])
            for n in range(4):
                si = (tt % 2) * 4 + n
                if need_out:
                    A("act", lambda e, si=si: e.copy(out=Sb[:, si, :, :], in_=S[:]), reads=[("S", h) for h in range(4)], writes=[("Sb", si)])
                bd = pbank("A")

                def fnd(e, bd=bd, n=n, tt=tt):
                    ins = None
                    for h in range(4):
                        ins = e.matmul(ps[bd][:, h * P:(h + 1) * P], lhsT=kendT[n * 32:(n + 1) * 32, tt, h * P:(h + 1) * P], rhs=vT[n * 32:(n + 1) * 32, tt, h * P:(h + 1) * P],
                                       start=True, stop=True, tile_position=(n * 32, 0))
                    return ins
                A("pe", fnd, reads=[("kendT", tt), ("vT", tt)], writes=[KP(bd)])
                for h in range(4):
                    A("dve", lambda e, bd=bd, h=h, cn=tt * 4 + n: e.scalar_tensor_tensor(out=S[:, h, :], in0=S[:, h, :], scalar=dcy[:, h, cn:cn + 1], in1=ps[bd][:, h * P:(h + 1) * P], op0=ALU.mult, op1=ALU.add),
                      reads=[("S", h), KP(bd), ("dcy", h)], writes=[("S", h)])

        def outs(tt):
            ts_ = slice(tt * P, (tt + 1) * P)
            bsc = pbank("C")

            def fnsc(e, bsc=bsc, ts_=ts_):
                ins = None
                for h in range(4):
                    ins = e.matmul(ps[bsc][:, h * P:(h + 1) * P], lhsT=G[:, 4 + h, ts_], rhs=G[:, h, ts_], start=True, stop=True)
                return ins
            A("pe", fnsc, reads=[KG(h) for h in range(8)], writes=[KP(bsc)])
            scm = TF[:, 8, :].bitcast(BF16)[:, 0:T]
            A("dve", lambda e, bsc=bsc, scm=scm: e.tensor_tensor(out=scm.rearrange("p (h t) -> p h t", h=4), in0=ps[bsc][:].rearrange("p (h t) -> p h t", h=4),
                                                              in1=cmaskb[:].unsqueeze(1).to_broadcast([P, 4, P]), op=ALU.mult),
              reads=[KP(bsc), "cmaskb"], writes=[KT_(8)])
            bo = pbank("C")

            def fno(e, bo=bo, tt=tt):
                ins = None
                for h in range(4):
                    ins = e.matmul(ps[bo][:, h * P:(h + 1) * P], lhsT=vT[:, tt, h * P:(h + 1) * P], rhs=scm[:, h * P:(h + 1) * P], start=(h == 0), stop=False, skip_group_check=True)
                for n in range(4):
                    si = (tt % 2) * 4 + n
                    for h in range(4):
                        ins = e.matmul(ps[bo][:, h * P + n * 32:h * P + (n + 1) * 32], lhsT=Sb[:, si, h, :], rhs=G[:, h, tt * P + n * 32:tt * P + (n + 1) * 32],
                                       start=False, stop=(n == 3), skip_group_check=True)
                return ins
            A("pe", fno, reads=[("vT", tt), KT_(8)] + [("Sb", (tt % 2) * 4 + n) for n in range(4)] + [KG(h) for h in range(4)], writes=[KP(bo)])
            A("act", lambda e, bo=bo: e.copy(out=TF[:, 9, :], in_=ps[bo][:]), reads=[KP(bo)], writes=[KT_(9)])
            A("act", lambda e: e.activation(out=G[:, 20, :], in_=TF[:, 9, :], func=AF.Square), reads=[KT_(9)], writes=[KG(20)])
            bss = pbank("B")
            A("pe", lambda e, bss=bss: e.matmul(ps[bss][:], lhsT=onesb[:, 2, :], rhs=G[:, 20, :], start=True, stop=True), reads=[KG(20), "ones2"], writes=[KP(bss)])
            A("dve", lambda e, bss=bss: e.tensor_scalar(out=TF[:, 10, :], in0=ps[bss][:], scalar1=float(RMS_EPS), scalar2=None, op0=ALU.add), reads=[KP(bss)], writes=[KT_(10)])
            A("act", lambda e: e.activation(out=TF[:, 10, :], in_=TF[:, 10, :], func=AF.Ln), reads=[KT_(10)], writes=[KT_(10)])
            A("act", lambda e: e.activation(out=TF[:, 10, :], in_=TF[:, 10, :], func=AF.Exp, scale=-0.5), reads=[KT_(10)], writes=[KT_(10)])
            A("dve", lambda e: e.tensor_tensor(out=TF[:, 9, :], in0=TF[:, 9, :], in1=TF[:, 10, :], op=ALU.mult), reads=[KT_(9), KT_(10)], writes=[KT_(9)])
            A("dve", lambda e, ts_=ts_: e.tensor_tensor(out=G[:, 16:20, ts_], in0=TF[:, 9, :].rearrange("p (h t) -> p h t", h=4), in1=G[:, 12:16, ts_], op=ALU.mult),
              reads=[KT_(9)] + [KG(12 + h) for h in range(4)], writes=[KG(16 + h) for h in range(4)])

        chain(0)
        for tt in range(4):
            if need_out and len(pieces) > 4:
                pieces.pop(0)()
            if tt + 1 < 4:
                chain(tt + 1)
            if pieces:
                pieces.pop(0)()
            if need_out:
                outs(tt)
        while pieces:
            pieces.pop(0)()
        if need_out:
            zc = lambda c: TF[:, c, :]
            zk = [KT_(c) for c in range(4)]
            rstd, nmr = ln_stats(4, T, zc, zk, 1, LN_EPS, s, pool="C")
            ln_norm(4, T, zc, zk, rstd, nmr, s)
            for cc in range(4):
                A("act", lambda e, cc=cc: e.activation(out=G[:, 20 + cc, :], in_=TF[:, cc, :], func=AF.Silu, scale=cfc("cvg", cc, 1), bias=cfc("cvbeta", cc, 1)),
                  reads=[KT_(cc), "cf"], writes=[KG(20 + cc)])
            proj_fm("wout", KC, list(range(KC)), lambda kc: G[:, 16 + kc, :], [KG(16 + c) for c in range(KC)], evac_resid(s))
            zb_cast(s)
            ln_stream("lnmix", 0, s)

    en = set(enable)
    if "xa0" in en or "xa1" in en:
        mem_kv()
    units = []
    if use_prefix and "mix0" in en:
        for g in range(NG - 1):
            def light(s, g=g):
                load_x(xprev, g * T, False, s)
                mix0(False, g == NG - 2, s)
            light(g % 2)
        def w_first(s):
            load_x(xprev, (NG - 1) * T, True, s)
            mix0(True, True, s)
        wst = [w_first]
        if "xa0" in en:
            wst.append(lambda s: xattn(0, s))

        def w_last(s):
            if "ffn0" in en:
                ffn(0, s)
            flush()
            A("dve", lambda e: e.tensor_scalar(out=phalo[:], in0=xres[:, s, :, T - 16:T], scalar1=cfc("flag"), scalar2=None, op0=ALU.mult),
              reads=[KX(s, c) for c in range(KC)] + ["cf"], writes=[("phalo", c) for c in range(KC)])
        wst.append(w_last)
        units.append(wst)
    for g in range(NG):
        names = []
        for l in range(2):
            for nm_ in ("mix", "xa", "ffn"):
                if f"{nm_}{l}" in en:
                    names.append((nm_, l))

        def mk_stage(g, idx, names):
            def stage(s):
                if idx == 0:
                    load_x(xcur, g * T, True, s)
                if names:
                    nm_, l = names[idx]
                    if nm_ == "mix" and l == 0:
                        mix0(True, True, s)
                    elif nm_ == "mix":
                        mix1(g == 0, s)
                    elif nm_ == "xa":
                        xattn(l, s)
                    elif l == 1:
                        ffn(l, s, store_row0=g * T)
                    else:
                        ffn(l, s)
                if "mix1" in en and idx + 1 < len(names) and names[idx + 1] == ("mix", 1):
                    mix1_chain(g == 0, s)
                if idx == max(len(names) - 1, 0) and "ffn1" not in en:
                    flush()
                    for c in range(KC):
                        A("act", lambda e, c=c: e.mul(out=xres[:, s, c, :], in_=xres[:, s, c, :], mul=1.0 / float(ALPHA)), reads=[KX(s, c)], writes=[KX(s, c)])
                    store_out(g * T, s)
            return stage
        units.append([mk_stage(g, i, names) for i in range(max(len(names), 1))])
    pipelined = not os.environ.get("MK_NOPIPE")
    pending = list(units)
    active = []
    free_streams = [0, 1]

    def refill():
        while pending and len(active) < (2 if pipelined else 1):
            active.append([pending.pop(0), 0, free_streams.pop(0)])
    refill()
    last = None
    while active:
        cand = [e_ for e_ in active if e_ is not last] or active
        ent = cand[0]
        stages_, idx_, sidx_ = ent
        if dq["q"] and dq["stream"] == sidx_:
            flush()
        n0_ = len(pg.ops)
        stages_[idx_](sidx_)
        if os.environ.get("MK_DUMP"):
            print("MKSTAGE stream", sidx_, "stage", idx_, "ops", n0_, "->", len(pg.ops), "queued", len(dq["q"]))
        ent[1] += 1
        last = ent
        if ent[1] == len(stages_):
            active.remove(ent)
            free_streams.append(sidx_)
            refill()
    flush()
    pg.add("sp", lambda e: None, reads=out_ops)
    nmax = int(os.environ.get("MK_MAXOPS", "0"))
    if nmax:
        for i_, op in enumerate(pg.ops[:nmax][-5:]):
            print("MK op", nmax - 5 + i_, op.eng, op.line)
        pg.ops = pg.ops[:nmax]
        sk = os.environ.get("MK_SKIP")
        if sk:
            pg.ops = [o for i_, o in enumerate(pg.ops) if i_ != int(sk)]
    print("MK total ops", len(pg.ops))
    pg.finalize()
    if os.environ.get("MK_DUMP"):
        for i_, op in enumerate(pg.ops):
            print("MKD", i_, op.eng, op.stream, op.sidx, op.line, "sig" if op.signal else "", [(s_, i2, pg.val[s_][i2]) for s_, i2 in op.waits])
    pg.emit(nc, es)
    es.close()
    nc._used_inputs = set(WSHAPE[k][0] for k in W)
    return nc


def make_cf(inputs, half, first_tokens_special):
    cfa = np.zeros((P, NCF), np.float32)

    def put(name, arr):
        o, n = CF_COLS[name]
        assert arr.shape == (P, n), (name, arr.shape, n)
        cfa[:, o:o + n] = arr
    put("ident", np.eye(P, dtype=np.float32))
    sm = np.ones((P, T), np.float32)
    sm[:, ::32] = 0.0
    put("scanmask", sm)
    s_ = np.arange(P)[:, None]
    t_ = np.arange(P)[None, :]
    put("cmask", ((s_ // 32 == t_ // 32) & (s_ <= t_)).astype(np.float32))
    ic = np.zeros((P, 64), np.float32)
    for gi in range(4):
        w = 2 << gi
        pos = np.arange(1, 17, dtype=np.float32)
        if first_tokens_special:
            ic[:, gi * 16:(gi + 1) * 16] = (1.0 / np.minimum(pos, float(w)))[None, :]
        else:
            ic[:, gi * 16:(gi + 1) * 16] = 1.0 / w
    put("invcnt", ic)
    put("flag", np.full((P, 1), float(half), np.float32))
    put("lbp", _vec_cols(inputs["lb_param"]))
    put("memg", _vec_cols(inputs["mem_ln_g"]))
    put("memb", _vec_cols(inputs["mem_ln_b"]))
    put("hgn", _vec_cols(inputs["hg_norm_g"][0]))
    cvw = np.asarray(inputs["cv_w"], np.float32)[0, :, 0, :]
    put("cvw", np.ascontiguousarray(cvw.reshape(CVK, 4, P).transpose(2, 1, 0).reshape(P, 4 * CVK)))
    put("cvb", _vec_cols(inputs["cv_b"][0]))
    put("cvg", _vec_cols(inputs["cv_ln_g"][0]))
    put("cvbeta", _vec_cols(inputs["cv_ln_b"][0]))
    put("pscale", _vec_cols(inputs["pool_scale"][0]))
    put("lnmix_g", _vec_cols(inputs["ln_mix_g"]))
    put("lnmix_b", _vec_cols(inputs["ln_mix_b"]))
    put("lnxa_g", _vec_cols(inputs["ln_xa_g"]))
    put("lnxa_b", _vec_cols(inputs["ln_xa_b"]))
    put("lnffn_g", _vec_cols(inputs["ln_ffn_g"]))
    put("lnffn_b", _vec_cols(inputs["ln_ffn_b"]))
    return cfa


_NC_CACHE = {}


def run(inputs, NG, enable=("mix0", "xa0", "ffn0", "mix1", "xa1", "ffn1"), trace=False):
    inputs = {k: np.asarray(v) for k, v in inputs.items()}
    x = inputs["x"].astype(np.float32, copy=False)
    B, S, _ = x.shape
    half_len = NG * T
    assert S == 2 * half_len
    key = (NG, tuple(enable))
    if key not in _NC_CACHE:
        _NC_CACHE[key] = build(NG, enable)
    nc = _NC_CACHE[key]
    f32 = lambda a: np.ascontiguousarray(np.asarray(a, np.float32))
    shared = {
        "w_in": f32(inputs["ab_w_in"][0]),
        "w_out": f32(inputs["ab_w_out"][0]),
        "pool_w": f32(inputs["pool_w"][0].reshape(D, 256)),
    }
    for l in range(2):
        shared[f"wq{l}"] = f32(inputs["xa_wq"][l])
        shared[f"wk{l}"] = f32(inputs["xa_wk"][l])
        shared[f"wv{l}"] = f32(inputs["xa_wv"][l])
        shared[f"wo{l}"] = f32(inputs["xa_wo"][l])
        shared[f"wg{l}"] = f32(inputs["ffn_wg"][l])
        shared[f"wu{l}"] = f32(inputs["ffn_wu"][l])
        shared[f"wd{l}"] = f32(inputs["ffn_wd"][l])
    ncores = int(os.environ.get("MK_CORES", 2 * B))
    in_maps = []
    for i in range(ncores):
        b, half = i // 2, i % 2
        m = {k: v for k, v in shared.items() if k in nc._used_inputs}
        m["xcur"] = f32(x[b, half * half_len:(half + 1) * half_len])
        m["xprev"] = f32(x[b, 0:half_len]) if half == 1 else np.zeros((half_len, D), np.float32)
        m["mem"] = f32(inputs["mem"][b])
        m["cf"] = make_cf(inputs, half, half == 0)
        in_maps.append(m)
    res = run_bass_kernel_spmd(nc, in_maps, core_ids=list(range(ncores)), **({"trace": True} if trace else {}))
    out = np.empty((B, S, D), np.float32)
    for i in range(ncores):
        b, half = i // 2, i % 2
        out[b, half * half_len:(half + 1) * half_len] = res.results[i]["out"]
    return out, res


def kernel(**inputs):
    out, _ = run(inputs, NG_FULL)
    return out
```

```python
import os
import sys
import numpy as np
from contextlib import ExitStack
import concourse.bass as bass
import concourse.mybir as mybir
from concourse.bass_utils import run_bass_kernel_spmd

F32 = mybir.dt.float32
BF16 = mybir.dt.bfloat16
AF = mybir.ActivationFunctionType
ALU = mybir.AluOpType

P = 128
D = 1024
KC = 8
T = 512
DFF = 2816
NJ = 22
MEM = 256
ALPHA = 4.0 ** 0.25
LN_EPS = 1e-5
RMS_EPS = 1e-6
CVK = 31
HALO = 30
SLOT = 2048
NSLOT = 7
NSTG = 1
NIO = 3
NTF = 11
NG_FULL = 8


class Op:
    __slots__ = ("eng", "fn", "dma", "npieces", "deps", "stream", "sidx", "signal", "waits", "done", "line")

    def __init__(self, eng, fn, dma, npieces):
        self.eng = eng
        self.fn = fn
        self.dma = dma
        self.npieces = npieces
        self.signal = False


class Prog:
    def __init__(self):
        self.ops = []
        self.lastw = {}
        self.readers = {}

    def add(self, eng, fn, reads=(), writes=(), dma=None, npieces=1):
        op = Op(eng, fn, dma, npieces)
        op.stream = dma if dma is not None else eng
        f_ = sys._getframe(1)
        if f_.f_code.co_name == "A":
            f_ = f_.f_back
        op.line = f_.f_lineno
        deps = set()
        psr = [k for k in reads if isinstance(k, tuple) and k[0] == "ps"]
        if psr:
            reads = [k for k in reads if k not in psr]
            writes = list(writes) + psr
        for k in reads:
            w = self.lastw.get(k)
            if w is not None:
                deps.add(w)
        for k in writes:
            w = self.lastw.get(k)
            if w is not None:
                deps.add(w)
            rd = self.readers.get(k)
            if rd:
                deps.update(rd.values())
        for k in reads:
            self.readers.setdefault(k, {})[op.stream] = op
        for k in writes:
            self.lastw[k] = op
            self.readers[k] = {}
        deps.discard(op)
        op.deps = deps
        self.ops.append(op)
        return op

    def finalize(self):
        streams = {}
        for op in self.ops:
            lst = streams.setdefault(op.stream, [])
            op.sidx = len(lst)
            lst.append(op)
        clocks = {}
        for op in self.ops:
            clk = clocks.setdefault(op.eng, {})
            need = {}
            for d in op.deps:
                if d.stream == op.eng and op.eng == "pe":
                    continue
                if clk.get(d.stream, -1) < d.sidx:
                    if need.get(d.stream, -1) < d.sidx:
                        need[d.stream] = d.sidx
            op.waits = sorted(need.items())
            for s_, i_ in op.waits:
                dop = streams[s_][i_]
                dop.signal = True
                for k2, v2 in dop.done.items():
                    if clk.get(k2, -1) < v2:
                        clk[k2] = v2
            done = dict(clk)
            done[op.stream] = op.sidx
            op.done = done
            if op.dma is None and op.eng == "pe":
                clk["pe"] = op.sidx
        if os.environ.get("MK_ALLSIG"):
            for op in self.ops:
                if op.dma is None and op.eng != "sp":
                    op.signal = True
        self.val = {}
        for s_, lst in streams.items():
            cnt = 0
            vals = []
            for op in lst:
                if op.dma is not None:
                    cnt += 16 * op.npieces
                elif op.signal:
                    cnt += 1
                vals.append(cnt)
            self.val[s_] = vals
        self.streams = streams
        for op in self.ops:
            op.done = None

    def emit(self, nc, es):
        sems = {}
        for s_ in self.streams:
            sems[s_] = es.enter_context(nc.semaphore("sem_" + s_))
            if os.environ.get("MK_DUMP"):
                print("MKSEM", s_, sems[s_].num)
        blk = es.enter_context(nc.Block())
        eng_ops = {}
        for op in self.ops:
            eng_ops.setdefault(op.eng, []).append(op)

        def mk(eng):
            def body(e):
                for op in eng_ops.get(eng, []):
                    for s_, i_ in op.waits:
                        e.wait_ge(sems[s_], self.val[s_][i_])
                    if op.dma is not None:
                        op.fn(e, sems[op.dma])
                    else:
                        ins = op.fn(e)
                        if op.signal:
                            ins.then_inc(sems[eng], 1)
            return body

        blk.sync(mk("sp"))
        blk.tensor(mk("pe"))
        blk.scalar(mk("act"))
        blk.vector(mk("dve"))
        blk.gpsimd(mk("pool"))


def _cf_layout():
    cols = {}
    off = 0
    for name, n in [("ident", 128), ("scanmask", 512), ("cmask", 128), ("invcnt", 64), ("flag", 1),
                    ("lbp", 12), ("memg", 8), ("memb", 8), ("hgn", 4), ("cvw", 124), ("cvb", 4),
                    ("cvg", 4), ("cvbeta", 4), ("pscale", 8),
                    ("lnmix_g", 16), ("lnmix_b", 16), ("lnxa_g", 16), ("lnxa_b", 16),
                    ("lnffn_g", 16), ("lnffn_b", 16)]:
        cols[name] = (off, n)
        off += n
    return cols, off


CF_COLS, NCF = _cf_layout()


def _vec_cols(v):
    v = np.asarray(v, np.float32)
    if v.ndim == 1:
        v = v[None]
    L = v.shape[0]
    n = v.shape[1] // 128
    return np.ascontiguousarray(v.reshape(L, n, 128).transpose(2, 0, 1).reshape(128, L * n))


def build(NG, enable=("mix0", "xa0", "ffn0", "mix1", "xa1", "ffn1"), use_prefix=True):
    nc = bass.Bass("TRN2", target_bir_lowering=False)
    NTOK = NG * T
    dt = lambda name, shape, kind="ExternalInput", dtype=F32: nc.dram_tensor(name, shape, dtype, kind=kind).ap()
    xcur = dt("xcur", [NTOK, D])
    xprev = dt("xprev", [NTOK, D])
    memd = dt("mem", [MEM, D])
    cfd = dt("cf", [P, NCF])
    WSHAPE = {"win": ("w_in", [D, 3072]), "wout": ("w_out", [D, D]), "pw": ("pool_w", [D, 256])}
    for l in range(2):
        for nm_, shp in [("wq", [D, D]), ("wk", [D, D]), ("wv", [D, D]), ("wo", [D, D]), ("wg", [D, DFF]), ("wu", [D, DFF]), ("wd", [DFF, D])]:
            WSHAPE[f"{nm_}{l}"] = (f"{nm_}{l}", shp)

    class _W(dict):
        def __missing__(self, k):
            nm_, shp = WSHAPE[k]
            self[k] = dt(nm_, shp)
            return self[k]
    W = _W()
    outd = dt("out", [NTOK, D], kind="ExternalOutput")
    NPAN = 200 if len(enable) > 0 else 1
    scr = dt("wscr", [NPAN, P, SLOT], kind="Internal", dtype=BF16)

    pg = Prog()
    es = ExitStack()
    E = es.enter_context
    sb = lambda name, shape, dtype=F32: E(nc.sbuf_tensor("sb_" + name, shape, dtype))

    io = sb("io", [P, NIO, D])
    cf = sb("cf", [P, NCF])
    dv = sb("dv", [P, 128])
    xres = sb("xres", [P, 2, KC, T])
    xb = sb("xb", [P, 2, KC, T], BF16)
    TFL = sb("TFL", [P, 2, 2, T])
    G = sb("G", [P, 24, T], BF16)
    TF = sb("TF", [P, NTF, T])
    kendT = sb("kendT", [P, 4, T], BF16)
    vT = sb("vT", [P, 4, T], BF16)
    ubuf = sb("ubuf", [P, 4, HALO + T], BF16)
    diag = sb("diag", [P, CVK, P], BF16)
    S = sb("S", [P, 4, P])
    Sb = sb("Sb", [P, 8, 4, P], BF16)
    dcy = sb("dcy", [P, 4, 16])
    KT = sb("KT", [P, 2, KC, MEM], BF16)
    Vm = sb("Vm", [P, 2, 2, D], BF16)
    slots = sb("slots", [P, NSLOT, SLOT], BF16)
    ptmp = sb("ptmp", [P, 2, 16 + T])
    P8 = sb("P8", [P, KC, T], BF16)
    phalo = sb("phalo", [P, KC, 16])
    identb = sb("identb", [P, P], BF16)
    cmaskb = sb("cmaskb", [P, P], BF16)
    onesb = sb("onesb", [P, 4, P], BF16)
    ps = [E(nc.psum_tensor(f"ps{i}", [P, T], F32)) for i in range(8)]

    def cfc(name, a=0, n=None):
        o, w = CF_COLS[name]
        if n is None:
            n = w - a
        return cf[:, o + a:o + a + n]

    ident = cfc("ident")

    rr = {"A": 0, "B": 0, "C": 0, "L": 0}
    pools = {"A": [0, 1, 2, 7], "L": [3, 4], "C": [5, 6]}

    dq = {"on": False, "q": [], "stream": None}

    def pump(n):
        if dq["on"]:
            return
        for _ in range(n):
            if not dq["q"]:
                return
            eng, fn, reads, writes, dma = dq["q"].pop(0)
            pg.add(eng, fn, reads, writes, dma=dma)

    def flush():
        pump(10 ** 9)

    def pbank(pool):
        pump(2)
        if pool == "B":
            pool = "C"
        lst = pools[pool]
        b = lst[rr[pool] % len(lst)]
        rr[pool] += 1
        return b

    def A(eng, fn, reads=(), writes=(), dma=None):
        if dq["on"]:
            dq["q"].append((eng, fn, list(reads), list(writes), dma))
            return None
        return pg.add(eng, fn, reads, writes, dma=dma)

    def KX(s, c):
        return ("xres", s, c)

    def KB(s, c):
        return ("xb", s, c)

    def KG(i):
        return ("G", i)

    def KT_(i):
        return ("TF", i)

    def KP(b):
        return ("ps", b)

    wstate = {"slot": 0, "stg": 0, "gid": {}, "scrw": 0}

    def get_panel(wname, kc0, nk, col0, ncols, to_scratch=True):
        n = nk * ncols
        assert n <= SLOT
        key = (wname, kc0, nk, col0, ncols)
        sl = wstate["slot"] % NSLOT
        wstate["slot"] += 1
        dst = slots[:, sl, 0:n]
        view = dst.rearrange("p (k n) -> p k n", k=nk)
        if key in wstate["gid"]:
            gid = wstate["gid"][key]
            pg.add("sp", lambda e, sem, dst=dst, gid=gid, n=n: e.dma_start(out=dst, in_=scr[gid, :, 0:n]).then_inc(sem, 16),
                   reads=[("scr", gid)], writes=[("w", sl)], dma=f"w{sl}")
        else:
            gid = len(wstate["gid"])
            assert gid < NPAN
            wstate["gid"][key] = gid
            src = W[wname][kc0 * P:(kc0 + nk) * P, col0:col0 + ncols].rearrange("(k p) n -> p k n", p=P)
            pg.add("pool", lambda e, sem, view=view, src=src: e.dma_start(out=view, in_=src).then_inc(sem, 16),
                   writes=[("w", sl)], dma=f"wl{sl}")
            if to_scratch:
                pg.add("sp", lambda e, sem, dst=dst, gid=gid, n=n: e.dma_start(out=scr[gid, :, 0:n], in_=dst).then_inc(sem, 16),
                       reads=[("w", sl)], writes=[("scr", gid)], dma=f"sw{sl}")
        return sl, view

    def proj_fm(wname, nk, col_chunks, act_ap, act_keys, evac, N=T, kc_base=0, cols_per_panel=256):
        cpp = cols_per_panel // P
        i = 0
        while i < len(col_chunks):
            grp = [col_chunks[i]]
            while len(grp) < cpp and i + len(grp) < len(col_chunks) and col_chunks[i + len(grp)] == grp[-1] + 1:
                grp.append(col_chunks[i + len(grp)])
            sl, view = get_panel(wname, kc_base, nk, grp[0] * P, len(grp) * P)
            for gi_, oc in enumerate(grp):
                b = pbank("A")

                def fn(e, b=b, view=view, gi_=gi_):
                    ins = None
                    for kc in range(nk):
                        ins = e.matmul(ps[b][:, 0:N], lhsT=view[:, kc, gi_ * P:(gi_ + 1) * P], rhs=act_ap(kc),
                                       start=(kc == 0), stop=(kc == nk - 1))
                    return ins
                A("pe", fn, reads=[("w", sl)] + list(act_keys), writes=[KP(b)])
                evac(i + gi_, oc, b)
            i += len(grp)

    pg.add("sp", lambda e, sem: e.dma_start(out=cf[:], in_=cfd).then_inc(sem, 16), writes=["cf"], dma="cfl")
    A("pool", lambda e: e.memset(onesb[:, 0, :], 1.0 / 1024), writes=["ones0"])
    A("pool", lambda e: e.memset(onesb[:, 1, :], 1.0 / 512), writes=["ones1"])
    A("pool", lambda e: e.memset(onesb[:, 2, :], 1.0 / 128), writes=["ones2"])
    A("pool", lambda e: e.memset(onesb[:, 3, :], 1.0), writes=["ones3"])
    A("pool", lambda e: e.memset(S[:], 0.0), writes=[("S", h) for h in range(4)])
    A("pool", lambda e: e.memset(ubuf[:], 0.0), writes=[("ubuf", c) for c in range(4)])
    A("pool", lambda e: e.memset(phalo[:], 0.0), writes=[("phalo", c) for c in range(KC)])
    A("pool", lambda e: e.memset(ptmp[:], 0.0), writes=[("ptmp", i) for i in range(2)])
    A("dve", lambda e: e.tensor_copy(out=identb[:], in_=ident), reads=["cf"], writes=["identb"])
    A("dve", lambda e: e.tensor_copy(out=cmaskb[:], in_=cfc("cmask")), reads=["cf"], writes=["cmaskb"])
    A("act", lambda e: e.activation(out=dv[:, 20:32], in_=cfc("lbp"), func=AF.Exp), reads=["cf"], writes=["dvt"])
    A("dve", lambda e: e.tensor_tensor(out=dv[:, 0:4], in0=dv[:, 20:24], in1=dv[:, 24:28], op=ALU.add), reads=["dvt"], writes=["dv0"])
    A("dve", lambda e: e.tensor_tensor(out=dv[:, 0:4], in0=dv[:, 0:4], in1=dv[:, 28:32], op=ALU.add), reads=["dvt", "dv0"], writes=["dv0"])
    A("dve", lambda e: e.reciprocal(out=dv[:, 4:8], in_=dv[:, 0:4]), reads=["dv0"], writes=["dv1"])
    A("dve", lambda e: e.tensor_tensor(out=dv[:, 0:4], in0=dv[:, 20:24], in1=dv[:, 4:8], op=ALU.mult), reads=["dvt", "dv1", "dv0"], writes=["dv0"])
    A("dve", lambda e: e.tensor_scalar(out=dv[:, 4:8], in0=dv[:, 0:4], scalar1=-1.0, scalar2=1.0, op0=ALU.mult, op1=ALU.add), reads=["dv0", "dv1"], writes=["dv1"])
    A("dve", lambda e: e.tensor_scalar(out=dv[:, 8:12], in0=dv[:, 0:4], scalar1=-1.0, scalar2=None, op0=ALU.add), reads=["dv0"], writes=["dv2"])
    A("dve", lambda e: e.tensor_scalar(out=dv[:, 12:20], in0=cfc("pscale"), scalar1=1.0 / ALPHA, scalar2=None, op0=ALU.mult), reads=["cf"], writes=["dv3"])
    LNN = ["lnmix", "lnxa", "lnffn"]
    for i_, nm in enumerate(LNN):
        for l in range(2):
            o_ = 32 + 16 * (i_ * 2 + l)
            A("dve", lambda e, o_=o_, nm=nm, l=l: e.tensor_scalar(out=dv[:, o_:o_ + 8], in0=cfc(nm + "_g", 8 * l, 8), scalar1=ALPHA, scalar2=None, op0=ALU.mult), reads=["cf"], writes=[("dvln", o_)])
            A("dve", lambda e, o_=o_, nm=nm, l=l: e.tensor_scalar(out=dv[:, o_ + 8:o_ + 16], in0=cfc(nm + "_b", 8 * l, 8), scalar1=ALPHA, scalar2=None, op0=ALU.mult), reads=["cf"], writes=[("dvln", o_ + 8)])
    DVK = ["dv0", "dv1", "dv2", "dv3"] + [("dvln", 32 + 8 * i) for i in range(12)]

    def ln_cols(nm, l, scaled):
        if scaled:
            o_ = 32 + 16 * (LNN.index(nm) * 2 + l)
            return (lambda c: dv[:, o_ + c:o_ + c + 1]), (lambda c: dv[:, o_ + 8 + c:o_ + 9 + c])
        return (lambda c: cfc(nm + "_g", 8 * l + c, 1)), (lambda c: cfc(nm + "_b", 8 * l + c, 1))

    def ln_stats(nch, N, zc, zkeys, ones_idx, eps, s, zb_done=False, pool="L", defer=False):
        bs, bq = pbank(pool), pbank(pool)
        K12, K13 = ("TFL", s, 0), ("TFL", s, 1)
        for c in range(nch):
            if not zb_done:
                A("dve", lambda e, c=c: e.tensor_copy(out=G[:, c, 0:N], in_=zc(c)), reads=[zkeys[c]], writes=[KG(c)])
            A("act", lambda e, c=c: e.activation(out=G[:, 8 + c, 0:N], in_=zc(c), func=AF.Square), reads=[zkeys[c]], writes=[KG(8 + c)])
            A("pe", lambda e, c=c: e.matmul(ps[bs][:, 0:N], lhsT=onesb[:, ones_idx, :], rhs=G[:, c, 0:N], start=(c == 0), stop=(c == nch - 1)),
              reads=[KG(c), f"ones{ones_idx}"], writes=[KP(bs)])
            A("pe", lambda e, c=c: e.matmul(ps[bq][:, 0:N], lhsT=onesb[:, ones_idx, :], rhs=G[:, 8 + c, 0:N], start=(c == 0), stop=(c == nch - 1)),
              reads=[KG(8 + c), f"ones{ones_idx}"], writes=[KP(bq)])
        t0 = TFL[:, s, 0, 0:N]
        t1 = TFL[:, s, 1, 0:N]
        if defer:
            dq["on"] = True
            dq["stream"] = s
        A("act", lambda e: e.activation(out=t0, in_=ps[bs][:, 0:N], func=AF.Square), reads=[KP(bs)], writes=[K12])
        A("dve", lambda e: e.scalar_tensor_tensor(out=t0, in0=t0, scalar=-1.0, in1=ps[bq][:, 0:N], op0=ALU.mult, op1=ALU.add), reads=[K12, KP(bq)], writes=[K12])
        A("dve", lambda e: e.tensor_scalar(out=t0, in0=t0, scalar1=float(eps), scalar2=None, op0=ALU.add), reads=[K12], writes=[K12])
        A("act", lambda e: e.activation(out=t1, in_=t0, func=AF.Ln), reads=[K12], writes=[K13])
        A("act", lambda e: e.activation(out=t1, in_=t1, func=AF.Exp, scale=-0.5), reads=[K13], writes=[K13])
        A("dve", lambda e: e.scalar_tensor_tensor(out=t0, in0=ps[bs][:, 0:N], scalar=-1.0, in1=t1, op0=ALU.mult, op1=ALU.mult), reads=[KP(bs), K13, K12], writes=[K12])
        return t1, t0

    def ln_norm(nch, N, zc, zkeys, rstd, nmr, s):
        K12, K13 = ("TFL", s, 0), ("TFL", s, 1)
        for c in range(nch):
            A("dve", lambda e, c=c: e.tensor_tensor(out=zc(c), in0=zc(c), in1=rstd, op=ALU.mult), reads=[zkeys[c], K13], writes=[zkeys[c]])
            A("dve", lambda e, c=c: e.tensor_tensor(out=zc(c), in0=zc(c), in1=nmr, op=ALU.add), reads=[zkeys[c], K12], writes=[zkeys[c]])

    def ln_stream(nm, l, s, final=False):
        zc = lambda c: xres[:, s, c, :]
        zk = [KX(s, c) for c in range(KC)]
        flush()
        rstd, nmr = ln_stats(KC, T, zc, zk, 0, LN_EPS, s, zb_done=True, defer=True)
        ln_norm(KC, T, zc, zk, rstd, nmr, s)
        g1, b1 = ln_cols(nm, l, False)
        g2, b2 = ln_cols(nm, l, not final)
        for c in range(KC):
            A("act", lambda e, c=c: e.activation(out=xb[:, s, c, :], in_=xres[:, s, c, :], func=AF.Identity, scale=g1(c), bias=b1(c)),
              reads=[KX(s, c), "cf"], writes=[KB(s, c)])
            A("act", lambda e, c=c: e.activation(out=xres[:, s, c, :], in_=xres[:, s, c, :], func=AF.Identity, scale=g2(c), bias=b2(c)),
              reads=[KX(s, c), "cf"] + DVK, writes=[KX(s, c)])
        if final and final > 0:
            store_out(final - 1, s)
        dq["on"] = False

    def evac_resid(s, scale=None):
        def ev(i, oc, b):
            if scale is None:
                A("dve", lambda e: e.tensor_tensor(out=xres[:, s, oc, :], in0=ps[b][:], in1=xres[:, s, oc, :], op=ALU.add), reads=[KP(b), KX(s, oc)], writes=[KX(s, oc)])
            else:
                A("dve", lambda e: e.scalar_tensor_tensor(out=xres[:, s, oc, :], in0=ps[b][:], scalar=scale(oc), in1=xres[:, s, oc, :], op0=ALU.mult, op1=ALU.add),
                  reads=[KP(b), KX(s, oc)] + DVK, writes=[KX(s, oc)])
        return ev

    def zb_cast(s):
        for c in range(KC):
            A("dve", lambda e, c=c: e.tensor_copy(out=G[:, c, :], in_=xres[:, s, c, :]), reads=[KX(s, c)], writes=[KG(c)])

    iost = {"n": 0}

    def load_x(src, row0, need_res, s):
        for tt in range(4):
            r = iost["n"] % 2
            iost["n"] += 1
            pg.add("pool", lambda e, sem, r=r, tt=tt: e.dma_start(out=io[:, r, :], in_=src[row0 + tt * P:row0 + (tt + 1) * P, :]).then_inc(sem, 16),
                   writes=[("io", r)], dma=f"io{r}")
            for half in range(2):
                b = pbank("C")

                def fn(e, b=b, r=r, half=half):
                    ins = None
                    for j in range(4):
                        c = half * 4 + j
                        ins = e.transpose(ps[b][:, j * P:(j + 1) * P], io[:, r, c * P:(c + 1) * P], ident)
                    return ins
                A("pe", fn, reads=[("io", r), "cf"], writes=[KP(b)])
                src_v = ps[b][:].rearrange("p (j t) -> p j t", j=4)
                sl_ = slice(tt * P, (tt + 1) * P)
                cs = slice(half * 4, half * 4 + 4)
                A("dve", lambda e, src_v=src_v, cs=cs, sl_=sl_: e.tensor_copy(out=xb[:, s, cs, sl_], in_=src_v),
                  reads=[KP(b)], writes=[KB(s, c) for c in range(half * 4, half * 4 + 4)])
                if need_res:
                    A("act", lambda e, src_v=src_v, cs=cs, sl_=sl_: e.mul(out=xres[:, s, cs, sl_], in_=src_v, mul=float(ALPHA)),
                      reads=[KP(b)], writes=[KX(s, c) for c in range(half * 4, half * 4 + 4)])

    out_ops = []

    def store_out(row0, s):
        for tt in range(4):
            r = 2
            for half in range(2):
                b = pbank("L")

                def fn(e, b=b, half=half, tt=tt):
                    ins = None
                    for j in range(4):
                        c = half * 4 + j
                        ins = e.transpose(ps[b][:, j * P:(j + 1) * P], xres[:, s, c, tt * P:(tt + 1) * P], ident)
                    return ins
                A("pe", fn, reads=[KX(s, c) for c in range(half * 4, half * 4 + 4)] + ["cf"], writes=[KP(b)])
                eng = "act" if half == 0 else "dve"
                if eng == "act":
                    A("act", lambda e, b=b, r=r, half=half: e.copy(out=io[:, r, half * T:(half + 1) * T], in_=ps[b][:]), reads=[KP(b)], writes=[("io", r)])
                else:
                    A("dve", lambda e, b=b, r=r, half=half: e.tensor_copy(out=io[:, r, half * T:(half + 1) * T], in_=ps[b][:]), reads=[KP(b)], writes=[("io", r)])
            A("pool", lambda e, sem, r=r, tt=tt: e.dma_start(out=outd[row0 + tt * P:row0 + (tt + 1) * P, :], in_=io[:, r, :]).then_inc(sem, 16),
              reads=[("io", r)], writes=[("out", row0, tt)], dma=f"io{r}")
            out_ops.append(("out", row0, tt))

    def ffn(l, s, store_row0=None):
        xk = [KB(s, c) for c in range(KC)]
        for j0 in range(0, NJ, 2):
            slg, vg = get_panel(f"wg{l}", 0, KC, j0 * P, 2 * P)
            slu, vu = get_panel(f"wu{l}", 0, KC, j0 * P, 2 * P)
            for jj in range(2):
                j = j0 + jj
                bg, bu = pbank("A"), pbank("A")

                def fng(e, bg=bg, vg=vg, jj=jj):
                    ins = None
                    for kc in range(KC):
                        ins = e.matmul(ps[bg][:], lhsT=vg[:, kc, jj * P:(jj + 1) * P], rhs=xb[:, s, kc, :], start=(kc == 0), stop=(kc == KC - 1))
                    return ins

                def fnu(e, bu=bu, vu=vu, jj=jj):
                    ins = None
                    for kc in range(KC):
                        ins = e.matmul(ps[bu][:], lhsT=vu[:, kc, jj * P:(jj + 1) * P], rhs=xb[:, s, kc, :], start=(kc == 0), stop=(kc == KC - 1))
                    return ins
                A("pe", fng, reads=[("w", slg)] + xk, writes=[KP(bg)])
                A("pe", fnu, reads=[("w", slu)] + xk, writes=[KP(bu)])
                tix = (10, 0)[j % 2]
                A("act", lambda e, bg=bg, tix=tix: e.activation(out=TF[:, tix, :], in_=ps[bg][:], func=AF.Silu), reads=[KP(bg)], writes=[KT_(tix)])
                A("dve", lambda e, bu=bu, tix=tix, j=j: e.tensor_tensor(out=G[:, j, :], in0=ps[bu][:], in1=TF[:, tix, :], op=ALU.mult),
                  reads=[KP(bu), KT_(tix)], writes=[KG(j)])
        ev = evac_resid(s)
        for c in range(KC):
            b = pbank("A")
            for hf in range(2):
                sl, view = get_panel(f"wd{l}", hf * 11, 11, c * P, P)

                def fn(e, b=b, view=view, hf=hf):
                    ins = None
                    for k in range(11):
                        j = hf * 11 + k
                        ins = e.matmul(ps[b][:], lhsT=view[:, k, :], rhs=G[:, j, :], start=(j == 0), stop=(j == NJ - 1))
                    return ins
                A("pe", fn, reads=[("w", sl)] + [KG(hf * 11 + k) for k in range(11)], writes=[KP(b)])
            ev(c, c, b)
        zb_cast(s)
        ln_stream("lnffn", l, s, final=(0 if l == 0 else (store_row0 + 1 if store_row0 is not None else -1)))

    def xattn(l, s):
        xk = [KB(s, c) for c in range(KC)]

        def evq(i, oc, b):
            A("act", lambda e: e.mul(out=G[:, oc, :], in_=ps[b][:], mul=1.0 / 16.0), reads=[KP(b)], writes=[KG(oc)])
        proj_fm(f"wq{l}", KC, list(range(KC)), lambda kc: xb[:, s, kc, :], xk, evq)
        for h in range(4):
            pk = [16 + (h % 2) * 2, 17 + (h % 2) * 2]
            for mc in range(2):
                b = pbank("C")

                def fn(e, b=b, h=h, mc=mc):
                    ins = None
                    for dc in range(2):
                        ins = e.matmul(ps[b][:], lhsT=KT[:, l, 2 * h + dc, mc * P:(mc + 1) * P], rhs=G[:, 2 * h + dc, :], start=(dc == 0), stop=(dc == 1))
                    return ins
                A("pe", fn, reads=[("KT", l), KG(2 * h), KG(2 * h + 1)], writes=[KP(b)])
                A("act", lambda e, b=b, g_=pk[mc]: e.activation(out=G[:, g_, :], in_=ps[b][:], func=AF.Exp), reads=[KP(b)], writes=[KG(pk[mc])])
            bsum = pbank("B")

            def fns(e, bsum=bsum, pk=pk):
                ins = None
                for mc in range(2):
                    ins = e.matmul(ps[bsum][:], lhsT=onesb[:, 3, :], rhs=G[:, pk[mc], :], start=(mc == 0), stop=(mc == 1))
                return ins
            A("pe", fns, reads=[KG(pk[0]), KG(pk[1]), "ones3"], writes=[KP(bsum)])
            tix = (10, 0)[h % 2]
            A("act", lambda e, bsum=bsum, tix=tix: e.activation(out=TF[:, tix, :], in_=ps[bsum][:], func=AF.Ln), reads=[KP(bsum)], writes=[KT_(tix)])
            A("act", lambda e, tix=tix: e.activation(out=TF[:, tix, :], in_=TF[:, tix, :], func=AF.Exp, scale=-1.0), reads=[KT_(tix)], writes=[KT_(tix)])
            for dc in range(2):
                b = pbank("A")

                def fno(e, b=b, h=h, dc=dc, pk=pk):
                    ins = None
                    for mc in range(2):
                        ins = e.matmul(ps[b][:], lhsT=Vm[:, l, mc, (2 * h + dc) * P:(2 * h + dc + 1) * P], rhs=G[:, pk[mc], :], start=(mc == 0), stop=(mc == 1))
                    return ins
                A("pe", fno, reads=[("Vm", l), KG(pk[0]), KG(pk[1])], writes=[KP(b)])
                A("dve", lambda e, b=b, tix=tix, oc=2 * h + dc: e.tensor_tensor(out=G[:, 8 + oc, :], in0=ps[b][:], in1=TF[:, tix, :], op=ALU.mult),
                  reads=[KP(b), KT_(tix)], writes=[KG(8 + 2 * h + dc)])
        proj_fm(f"wo{l}", KC, list(range(KC)), lambda kc: G[:, 8 + kc, :], [KG(8 + c) for c in range(KC)], evac_resid(s))
        zb_cast(s)
        ln_stream("lnxa", l, s)

    def mem_kv():
        memT = lambda c: TF[:, c // 2, (c % 2) * MEM:(c % 2 + 1) * MEM]
        mk = [("memT", c) for c in range(KC)]
        for mt in range(2):
            r = iost["n"] % 2
            iost["n"] += 1
            pg.add("pool", lambda e, sem, r=r, mt=mt: e.dma_start(out=io[:, r, :], in_=memd[mt * P:(mt + 1) * P, :]).then_inc(sem, 16), writes=[("io", r)], dma=f"io{r}")
            for half in range(2):
                b = pbank("C")

                def fn(e, b=b, r=r, half=half):
                    ins = None
                    for j in range(4):
                        c = half * 4 + j
                        ins = e.transpose(ps[b][:, j * P:(j + 1) * P], io[:, r, c * P:(c + 1) * P], ident)
                    return ins
                A("pe", fn, reads=[("io", r), "cf"], writes=[KP(b)])
                for j in range(4):
                    c = half * 4 + j
                    A("dve", lambda e, b=b, j=j, c=c, mt=mt: e.tensor_copy(out=memT(c)[:, mt * P:(mt + 1) * P], in_=ps[b][:, j * P:(j + 1) * P]),
                      reads=[KP(b)], writes=[mk[c], KT_(c // 2)])
        rstd, nmr = ln_stats(KC, MEM, memT, mk, 0, LN_EPS, 0)
        ln_norm(KC, MEM, memT, mk, rstd, nmr, 0)
        mn = lambda c: G[:, 16 + c // 2, (c % 2) * MEM:(c % 2 + 1) * MEM]
        mnk = [KG(16 + c // 2) for c in range(KC)]
        for c in range(KC):
            A("dve", lambda e, c=c: e.tensor_scalar(out=mn(c), in0=memT(c), scalar1=cfc("memg", c, 1), scalar2=cfc("memb", c, 1), op0=ALU.mult, op1=ALU.add),
              reads=[mk[c], "cf"], writes=[mnk[c]])
        for l in range(2):
            def evk(i, oc, b, l=l):
                A("act", lambda e: e.copy(out=KT[:, l, oc, :], in_=ps[b][:, 0:MEM]), reads=[KP(b)], writes=[("KT", l)])
            proj_fm(f"wk{l}", KC, list(range(KC)), mn, list(set(mnk)), evk, N=MEM)
            for cp in range(4):
                sl, view = get_panel(f"wv{l}", 0, KC, cp * 256, 256, to_scratch=False)
                for mc in range(2):
                    b = pbank("A")

                    def fn(e, b=b, view=view, mc=mc):
                        ins = None
                        for kc in range(KC):
                            ins = e.matmul(ps[b][:, 0:256], lhsT=mn(kc)[:, mc * P:(mc + 1) * P], rhs=view[:, kc, :], start=(kc == 0), stop=(kc == KC - 1))
                        return ins
                    A("pe", fn, reads=[("w", sl)] + list(set(mnk)), writes=[KP(b)])
                    A("act", lambda e, b=b, mc=mc, cp=cp, l=l: e.copy(out=Vm[:, l, mc, cp * 256:(cp + 1) * 256], in_=ps[b][:, 0:256]), reads=[KP(b)], writes=[("Vm", l)])

    def mix1_chain(first_group, s):
        dq["on"] = True
        dq["stream"] = s
        for c in range(KC):
            gi = c // 2
            w = 2 << gi
            nsteps = gi + 1
            pb0 = 0
            bufs = [("ptmp", pb0), ("ptmp", pb0 + 1)]
            A("dve", lambda e, c=c, pb0=pb0: e.tensor_copy(out=ptmp[:, pb0, 0:16], in_=phalo[:, c, :]), reads=[("phalo", c)], writes=[bufs[0]])
            A("act", lambda e, c=c, pb0=pb0: e.copy(out=ptmp[:, pb0, 16:16 + T], in_=xres[:, s, c, :]), reads=[KX(s, c)], writes=[bufs[0]])
            A("dve", lambda e, c=c: e.tensor_copy(out=phalo[:, c, :], in_=xres[:, s, c, T - 16:T]), reads=[KX(s, c), bufs[0]], writes=[("phalo", c)])
            cur = 0
            sh = 1
            for s_ in range(nsteps):
                nxt = 1 - cur
                A("dve", lambda e, cur=cur, nxt=nxt, sh=sh, pb0=pb0: e.tensor_tensor(out=ptmp[:, pb0 + nxt, sh:16 + T], in0=ptmp[:, pb0 + cur, sh:16 + T], in1=ptmp[:, pb0 + cur, 0:16 + T - sh], op=ALU.add),
                  reads=[bufs[cur]], writes=[bufs[nxt]])
                cur = nxt
                sh *= 2
            A("dve", lambda e, cur=cur, c=c, w=w, pb0=pb0: e.scalar_tensor_tensor(out=P8[:, c, :], in0=ptmp[:, pb0 + cur, 16:16 + T], scalar=1.0 / w, in1=xres[:, s, c, :], op0=ALU.mult, op1=ALU.subtract),
              reads=[bufs[cur], KX(s, c)], writes=[("P8", c)])
            if first_group:
                A("dve", lambda e, cur=cur, gi=gi, pb0=pb0: e.tensor_tensor(out=ptmp[:, pb0 + cur, 16:32], in0=ptmp[:, pb0 + cur, 16:32], in1=cfc("invcnt", gi * 16, 16), op=ALU.mult),
                  reads=[bufs[cur], "cf"], writes=[bufs[cur]])
                A("dve", lambda e, cur=cur, c=c, pb0=pb0: e.tensor_tensor(out=P8[:, c, 0:16], in0=ptmp[:, pb0 + cur, 16:32], in1=xres[:, s, c, 0:16], op=ALU.subtract),
                  reads=[bufs[cur], KX(s, c)], writes=[("P8", c)])
        dq["on"] = False

    def mix1(first_group, s):
        ev = evac_resid(s, scale=lambda oc: dv[:, 12 + oc:13 + oc])
        for gi in range(4):
            sl, view = get_panel("pw", 2 * gi, 2, 0, 256)
            for oo in range(2):
                oc = 2 * gi + oo
                b = pbank("A")

                def fn(e, b=b, view=view, gi=gi, oo=oo):
                    ins = None
                    for kc in range(2):
                        ins = e.matmul(ps[b][:], lhsT=view[:, kc, oo * P:(oo + 1) * P], rhs=P8[:, 2 * gi + kc, :], start=(kc == 0), stop=(kc == 1))
                    return ins
                A("pe", fn, reads=[("w", sl), ("P8", 2 * gi), ("P8", 2 * gi + 1)], writes=[KP(b)])
                ev(oc, oc, b)
        zb_cast(s)
        ln_stream("lnmix", 1, s)

    sbst = {"n": 0}

    def mix0(need_out, need_u, s):
        xk = [KB(s, c) for c in range(KC)]
        act = lambda kc: xb[:, s, kc, :]
        for cp in range(2):
            sl, view = get_panel("win", 0, KC, 1024 + cp * 256, 256)
            for tt in range(4):
                b = pbank("A")

                def fn(e, b=b, view=view, tt=tt):
                    ins = None
                    for kc in range(KC):
                        ins = e.matmul(ps[b][:, 0:256], lhsT=xb[:, s, kc, tt * P:(tt + 1) * P], rhs=view[:, kc, :], start=(kc == 0), stop=(kc == KC - 1))
                    return ins
                A("pe", fn, reads=[("w", sl)] + xk, writes=[KP(b)])
                A("act", lambda e, b=b, tt=tt, cp=cp: e.copy(out=vT[:, tt, cp * 256:(cp + 1) * 256], in_=ps[b][:, 0:256]), reads=[KP(b)], writes=[("vT", tt)])
        def evf(i, h, b):
            sig, logf, kk, bb, eb, enb, blb = [TF[:, t_, :] for t_ in range(7)]
            A("act", lambda e: e.activation(out=sig, in_=ps[b][:], func=AF.Exp, scale=-1.0), reads=[KP(b)], writes=[KT_(0)])
            A("act", lambda e: e.activation(out=logf, in_=sig, func=AF.Ln, scale=dv[:, h:h + 1], bias=1.0), reads=[KT_(0)] + DVK, writes=[KT_(1)])
            A("act", lambda e: e.activation(out=enb, in_=sig, func=AF.Ln, bias=1.0), reads=[KT_(0)], writes=[KT_(5)])
            A("dve", lambda e: e.tensor_tensor(out=logf, in0=logf, in1=enb, op=ALU.subtract), reads=[KT_(1), KT_(5)], writes=[KT_(1)])
            A("act", lambda e: e.activation(out=enb, in_=enb, func=AF.Exp, scale=-1.0), reads=[KT_(5)], writes=[KT_(5)])
            A("dve", lambda e: e.scalar_tensor_tensor(out=kk, in0=sig, scalar=dv[:, 4 + h:5 + h], in1=enb, op0=ALU.mult, op1=ALU.mult), reads=[KT_(0), KT_(5)] + DVK, writes=[KT_(2)])
            A("dve", lambda e: e.tensor_tensor_scan(out=bb, data0=cfc("scanmask"), data1=logf, initial=0.0, op0=ALU.mult, op1=ALU.add), reads=[KT_(1), "cf"], writes=[KT_(3)])
            bv = bb.rearrange("p (n c) -> p n c", c=32)
            A("dve", lambda e: e.tensor_tensor(out=blb.rearrange("p (n c) -> p n c", c=32), in0=bv[:, :, 31:32].to_broadcast([P, 16, 32]), in1=bv, op=ALU.subtract),
              reads=[KT_(3)], writes=[KT_(6)])
            A("act", lambda e: e.activation(out=eb, in_=bb, func=AF.Exp), reads=[KT_(3)], writes=[KT_(4)])
            A("act", lambda e: e.activation(out=blb, in_=blb, func=AF.Exp), reads=[KT_(6)], writes=[KT_(6)])
            A("dve", lambda e: e.tensor_copy(out=dcy[:, h, :], in_=eb.rearrange("p (n c) -> p n c", c=32)[:, :, 31]), reads=[KT_(4)], writes=[("dcy", h)])
            A("dve", lambda e: e.tensor_tensor(out=G[:, 8 + h, :], in0=kk, in1=blb, op=ALU.mult), reads=[KT_(2), KT_(6)], writes=[KG(8 + h)])
            if need_out:
                A("act", lambda e: e.activation(out=enb, in_=bb, func=AF.Exp, scale=-1.0), reads=[KT_(3)], writes=[KT_(5)])
                A("dve", lambda e: e.tensor_tensor(out=G[:, 4 + h, :], in0=kk, in1=enb, op=ALU.mult), reads=[KT_(2), KT_(5)], writes=[KG(4 + h)])
                def evq(i2, oc2, b2):
                    A("dve", lambda e: e.tensor_tensor(out=G[:, h, :], in0=ps[b2][:], in1=eb, op=ALU.mult), reads=[KP(b2), KT_(4)], writes=[KG(h)])
                if h % 2 == 0:
                    qpan["p"] = get_panel("win", 0, KC, h * P, 2 * P)
                slq, vq = qpan["p"]
                bq_ = pbank("A")

                def fnq(e, bq_=bq_, vq=vq, gi_=h % 2):
                    ins = None
                    for kc in range(KC):
                        ins = e.matmul(ps[bq_][:], lhsT=vq[:, kc, gi_ * P:(gi_ + 1) * P], rhs=xb[:, s, kc, :], start=(kc == 0), stop=(kc == KC - 1))
                    return ins
                A("pe", fnq, reads=[("w", slq)] + xk, writes=[KP(bq_)])
                evq(0, h, bq_)
        qpan = {}
        proj_fm("win", KC, [4 + h for h in range(4)], act, xk, lambda i, oc, b: evf(i, oc - 4, b), cols_per_panel=256)
        pieces = []
        if need_out:
            def evg(i, oc, b):
                h = oc - 12
                A("act", lambda e: e.activation(out=TF[:, 7, :], in_=ps[b][:], func=AF.Silu), reads=[KP(b)], writes=[KT_(7)])
                A("dve", lambda e: e.tensor_scalar(out=G[:, 12 + h, :], in0=TF[:, 7, :], scalar1=cfc("hgn", h, 1), scalar2=None, op0=ALU.mult), reads=[KT_(7), "cf"], writes=[KG(12 + h)])
            pieces.append(lambda: proj_fm("win", KC, [12, 13], act, xk, evg))
            pieces.append(lambda: proj_fm("win", KC, [14, 15], act, xk, evg))
        def cpiece(cc):
            bga = pbank("A")
            sla, va = get_panel("win", 0, KC, (16 + cc) * P, P)
            slg, vg = get_panel("win", 0, KC, (20 + cc) * P, P)
            bgg = pbank("A")

            def fna(e, b=bga, v=va):
                ins = None
                for kc in range(KC):
                    ins = e.matmul(ps[b][:], lhsT=v[:, kc, :], rhs=xb[:, s, kc, :], start=(kc == 0), stop=(kc == KC - 1))
                return ins

            def fngt(e, b=bgg, v=vg):
                ins = None
                for kc in range(KC):
                    ins = e.matmul(ps[b][:], lhsT=v[:, kc, :], rhs=xb[:, s, kc, :], start=(kc == 0), stop=(kc == KC - 1))
                return ins
            A("pe", fna, reads=[("w", sla)] + xk, writes=[KP(bga)])
            A("pe", fngt, reads=[("w", slg)] + xk, writes=[KP(bgg)])
            A("act", lambda e, b=bgg: e.activation(out=TF[:, 7, :], in_=ps[b][:], func=AF.Sigmoid), reads=[KP(bgg)], writes=[KT_(7)])
            A("dve", lambda e, b=bga, cc=cc: e.tensor_tensor(out=ubuf[:, cc, HALO:HALO + T], in0=ps[b][:], in1=TF[:, 7, :], op=ALU.mult), reads=[KP(bga), KT_(7)], writes=[("ubuf", cc)])
            if need_out:
                A("pool", lambda e, cc=cc: e.tensor_tensor(out=diag[:], in0=identb[:].unsqueeze(1).to_broadcast([P, CVK, P]),
                                                           in1=cfc("cvw", cc * CVK, CVK).unsqueeze(2).to_broadcast([P, CVK, P]), op=ALU.mult),
                  reads=["identb", "cf"], writes=["diag"])
                bc = pbank("A")

                def fnc(e, bc=bc, cc=cc):
                    ins = None
                    for j in range(CVK):
                        ins = e.matmul(ps[bc][:], lhsT=diag[:, j, :], rhs=ubuf[:, cc, j:j + T], start=(j == 0), stop=(j == CVK - 1))
                    return ins
                A("pe", fnc, reads=["diag", ("ubuf", cc)], writes=[KP(bc)])
                A("act", lambda e, bc=bc, cc=cc: e.activation(out=TF[:, cc, :], in_=ps[bc][:], func=AF.Identity, bias=cfc("cvb", cc, 1)), reads=[KP(bc), "cf"], writes=[KT_(cc)])
            A("dve", lambda e, cc=cc: e.tensor_copy(out=ubuf[:, cc, 0:HALO], in_=ubuf[:, cc, T:T + HALO]), reads=[("ubuf", cc)], writes=[("ubuf", cc)])
        if need_u or need_out:
            for cc in range(4):
                pieces.append(lambda cc=cc: cpiece(cc))
        def chain(tt):
            ts_ = slice(tt * P, (tt + 1) * P)
            bt = pbank("A")
            ptv = ps[bt][:].bitcast(BF16)

            def fnt(e, ptv=ptv, ts_=ts_):
                ins = None
                for h in range(4):
                    ins = e.transpose(ptv[:, h * P:(h + 1) * P], G[:, 8 + h, ts_], identb[:])
                return ins
            A("pe", fnt, reads=[KG(8 + h) for h in range(4)] + ["identb"], writes=[KP(bt)])
            A("act", lambda e, ptv=ptv, tt=tt: e.copy(out=kendT[:, tt, :], in_=ptv[:, 0:T]), reads=[KP(bt)], writes=[("kendT", tt)])
            for n in range(4):
                si = (tt % 2) * 4 + n
                if need_out:
                    A("act", lambda e, si=si: e.copy(out=Sb[:, si, :, :], in_=S[:]), reads=[("S", h) for h in range(4)], writes=[("Sb", si)])
                bd = pbank("A")

                def fnd(e, bd=bd, n=n, tt=tt):
                    ins = None
                    for h in range(4):
                        ins = e.matmul(ps[bd][:, h * P:(h + 1) * P], lhsT=kendT[n * 32:(n + 1) * 32, tt, h * P:(h + 1) * P], rhs=vT[n * 32:(n + 1) * 32, tt, h * P:(h + 1) * P],
                                       start=True, stop=True, tile_position=(n * 32, 0))
                    return ins
                A("pe", fnd, reads=[("kendT", tt), ("vT", tt)], writes=[KP(bd)])
                for h in range(4):
                    A("dve", lambda e, bd=bd, h=h, cn=tt * 4 + n: e.scalar_tensor_tensor(out=S[:, h, :], in0=S[:, h, :], scalar=dcy[:, h, cn:cn + 1], in1=ps[bd][:, h * P:(h + 1) * P], op0=ALU.mult, op1=ALU.add),
                      reads=[("S", h), KP(bd), ("dcy", h)], writes=[("S", h)])

        def outs(tt):
            ts_ = slice(tt * P, (tt + 1) * P)
            bsc = pbank("C")

            def fnsc(e, bsc=bsc, ts_=ts_):
                ins = None
                for h in range(4):
                    ins = e.matmul(ps[bsc][:, h * P:(h + 1) * P], lhsT=G[:, 4 + h, ts_], rhs=G[:, h, ts_], start=True, stop=True)
                return ins
            A("pe", fnsc, reads=[KG(h) for h in range(8)], writes=[KP(bsc)])
            scm = TF[:, 8, :].bitcast(BF16)[:, 0:T]
            A("dve", lambda e, bsc=bsc, scm=scm: e.tensor_tensor(out=scm.rearrange("p (h t) -> p h t", h=4), in0=ps[bsc][:].rearrange("p (h t) -> p h t", h=4),
                                                              in1=cmaskb[:].unsqueeze(1).to_broadcast([P, 4, P]), op=ALU.mult),
              reads=[KP(bsc), "cmaskb"], writes=[KT_(8)])
            bo = pbank("C")

            def fno(e, bo=bo, tt=tt):
                ins = None
                for h in range(4):
                    ins = e.matmul(ps[bo][:, h * P:(h + 1) * P], lhsT=vT[:, tt, h * P:(h + 1) * P], rhs=scm[:, h * P:(h + 1) * P], start=(h == 0), stop=False, skip_group_check=True)
                for n in range(4):
                    si = (tt % 2) * 4 + n
                    for h in range(4):
                        ins = e.matmul(ps[bo][:, h * P + n * 32:h * P + (n + 1) * 32], lhsT=Sb[:, si, h, :], rhs=G[:, h, tt * P + n * 32:tt * P + (n + 1) * 32],
                                       start=False, stop=(n == 3), skip_group_check=True)
                return ins
            A("pe", fno, reads=[("vT", tt), KT_(8)] + [("Sb", (tt % 2) * 4 + n) for n in range(4)] + [KG(h) for h in range(4)], writes=[KP(bo)])
            A("act", lambda e, bo=bo: e.copy(out=TF[:, 9, :], in_=ps[bo][:]), reads=[KP(bo)], writes=[KT_(9)])
            A("act", lambda e: e.activation(out=G[:, 20, :], in_=TF[:, 9, :], func=AF.Square), reads=[KT_(9)], writes=[KG(20)])
            bss = pbank("B")
            A("pe", lambda e, bss=bss: e.matmul(ps[bss][:], lhsT=onesb[:, 2, :], rhs=G[:, 20, :], start=True, stop=True), reads=[KG(20), "ones2"], writes=[KP(bss)])
            A("dve", lambda e, bss=bss: e.tensor_scalar(out=TF[:, 10, :], in0=ps[bss][:], scalar1=float(RMS_EPS), scalar2=None, op0=ALU.add), reads=[KP(bss)], writes=[KT_(10)])
            A("act", lambda e: e.activation(out=TF[:, 10, :], in_=TF[:, 10, :], func=AF.Ln), reads=[KT_(10)], writes=[KT_(10)])
            A("act", lambda e: e.activation(out=TF[:, 10, :], in_=TF[:, 10, :], func=AF.Exp, scale=-0.5), reads=[KT_(10)], writes=[KT_(10)])
            A("dve", lambda e: e.tensor_tensor(out=TF[:, 9, :], in0=TF[:, 9, :], in1=TF[:, 10, :], op=ALU.mult), reads=[KT_(9), KT_(10)], writes=[KT_(9)])
            A("dve", lambda e, ts_=ts_: e.tensor_tensor(out=G[:, 16:20, ts_], in0=TF[:, 9, :].rearrange("p (h t) -> p h t", h=4), in1=G[:, 12:16, ts_], op=ALU.mult),
              reads=[KT_(9)] + [KG(12 + h) for h in range(4)], writes=[KG(16 + h) for h in range(4)])

        chain(0)
        for tt in range(4):
            if need_out and len(pieces) > 4:
                pieces.pop(0)()
            if tt + 1 < 4:
                chain(tt + 1)
            if pieces:
                pieces.pop(0)()
            if need_out:
                outs(tt)
        while pieces:
            pieces.pop(0)()
        if need_out:
            zc = lambda c: TF[:, c, :]
            zk = [KT_(c) for c in range(4)]
            rstd, nmr = ln_stats(4, T, zc, zk, 1, LN_EPS, s, pool="C")
            ln_norm(4, T, zc, zk, rstd, nmr, s)
            for cc in range(4):
                A("act", lambda e, cc=cc: e.activation(out=G[:, 20 + cc, :], in_=TF[:, cc, :], func=AF.Silu, scale=cfc("cvg", cc, 1), bias=cfc("cvbeta", cc, 1)),
                  reads=[KT_(cc), "cf"], writes=[KG(20 + cc)])
            proj_fm("wout", KC, list(range(KC)), lambda kc: G[:, 16 + kc, :], [KG(16 + c) for c in range(KC)], evac_resid(s))
            zb_cast(s)
            ln_stream("lnmix", 0, s)

    en = set(enable)
    if "xa0" in en or "xa1" in en:
        mem_kv()
    units = []
    if use_prefix and "mix0" in en:
        for g in range(NG - 1):
            def light(s, g=g):
                load_x(xprev, g * T, False, s)
                mix0(False, g == NG - 2, s)
            light(g % 2)
        def w_first(s):
            load_x(xprev, (NG - 1) * T, True, s)
            mix0(True, True, s)
        wst = [w_first]
        if "xa0" in en:
            wst.append(lambda s: xattn(0, s))

        def w_last(s):
            if "ffn0" in en:
                ffn(0, s)
            flush()
            A("dve", lambda e: e.tensor_scalar(out=phalo[:], in0=xres[:, s, :, T - 16:T], scalar1=cfc("flag"), scalar2=None, op0=ALU.mult),
              reads=[KX(s, c) for c in range(KC)] + ["cf"], writes=[("phalo", c) for c in range(KC)])
        wst.append(w_last)
        units.append(wst)
    for g in range(NG):
        names = []
        for l in range(2):
            for nm_ in ("mix", "xa", "ffn"):
                if f"{nm_}{l}" in en:
                    names.append((nm_, l))

        def mk_stage(g, idx, names):
            def stage(s):
                if idx == 0:
                    load_x(xcur, g * T, True, s)
                if names:
                    nm_, l = names[idx]
                    if nm_ == "mix" and l == 0:
                        mix0(True, True, s)
                    elif nm_ == "mix":
                        mix1(g == 0, s)
                    elif nm_ == "xa":
                        xattn(l, s)
                    elif l == 1:
                        ffn(l, s, store_row0=g * T)
                    else:
                        ffn(l, s)
                if "mix1" in en and idx + 1 < len(names) and names[idx + 1] == ("mix", 1):
                    mix1_chain(g == 0, s)
                if idx == max(len(names) - 1, 0) and "ffn1" not in en:
                    flush()
                    for c in range(KC):
                        A("act", lambda e, c=c: e.mul(out=xres[:, s, c, :], in_=xres[:, s, c, :], mul=1.0 / float(ALPHA)), reads=[KX(s, c)], writes=[KX(s, c)])
                    store_out(g * T, s)
            return stage
        units.append([mk_stage(g, i, names) for i in range(max(len(names), 1))])
    pipelined = not os.environ.get("MK_NOPIPE")
    pending = list(units)
    active = []
    free_streams = [0, 1]

    def refill():
        while pending and len(active) < (2 if pipelined else 1):
            active.append([pending.pop(0), 0, free_streams.pop(0)])
    refill()
    last = None
    while active:
        cand = [e_ for e_ in active if e_ is not last] or active
        ent = cand[0]
        stages_, idx_, sidx_ = ent
        if dq["q"] and dq["stream"] == sidx_:
            flush()
        n0_ = len(pg.ops)
        stages_[idx_](sidx_)
        if os.environ.get("MK_DUMP"):
            print("MKSTAGE stream", sidx_, "stage", idx_, "ops", n0_, "->", len(pg.ops), "queued", len(dq["q"]))
        ent[1] += 1
        last = ent
        if ent[1] == len(stages_):
            active.remove(ent)
            free_streams.append(sidx_)
            refill()
    flush()
    pg.add("sp", lambda e: None, reads=out_ops)
    nmax = int(os.environ.get("MK_MAXOPS", "0"))
    if nmax:
        for i_, op in enumerate(pg.ops[:nmax][-5:]):
            print("MK op", nmax - 5 + i_, op.eng, op.line)
        pg.ops = pg.ops[:nmax]
        sk = os.environ.get("MK_SKIP")
        if sk:
            pg.ops = [o for i_, o in enumerate(pg.ops) if i_ != int(sk)]
    print("MK total ops", len(pg.ops))
    pg.finalize()
    if os.environ.get("MK_DUMP"):
        for i_, op in enumerate(pg.ops):
            print("MKD", i_, op.eng, op.stream, op.sidx, op.line, "sig" if op.signal else "", [(s_, i2, pg.val[s_][i2]) for s_, i2 in op.waits])
    pg.emit(nc, es)
    es.close()
    nc._used_inputs = set(WSHAPE[k][0] for k in W)
    return nc


def make_cf(inputs, half, first_tokens_special):
    cfa = np.zeros((P, NCF), np.float32)

    def put(name, arr):
        o, n = CF_COLS[name]
        assert arr.shape == (P, n), (name, arr.shape, n)
        cfa[:, o:o + n] = arr
    put("ident", np.eye(P, dtype=np.float32))
    sm = np.ones((P, T), np.float32)
    sm[:, ::32] = 0.0
    put("scanmask", sm)
    s_ = np.arange(P)[:, None]
    t_ = np.arange(P)[None, :]
    put("cmask", ((s_ // 32 == t_ // 32) & (s_ <= t_)).astype(np.float32))
    ic = np.zeros((P, 64), np.float32)
    for gi in range(4):
        w = 2 << gi
        pos = np.arange(1, 17, dtype=np.float32)
        if first_tokens_special:
            ic[:, gi * 16:(gi + 1) * 16] = (1.0 / np.minimum(pos, float(w)))[None, :]
        else:
            ic[:, gi * 16:(gi + 1) * 16] = 1.0 / w
    put("invcnt", ic)
    put("flag", np.full((P, 1), float(half), np.float32))
    put("lbp", _vec_cols(inputs["lb_param"]))
    put("memg", _vec_cols(inputs["mem_ln_g"]))
    put("memb", _vec_cols(inputs["mem_ln_b"]))
    put("hgn", _vec_cols(inputs["hg_norm_g"][0]))
    cvw = np.asarray(inputs["cv_w"], np.float32)[0, :, 0, :]
    put("cvw", np.ascontiguousarray(cvw.reshape(CVK, 4, P).transpose(2, 1, 0).reshape(P, 4 * CVK)))
    put("cvb", _vec_cols(inputs["cv_b"][0]))
    put("cvg", _vec_cols(inputs["cv_ln_g"][0]))
    put("cvbeta", _vec_cols(inputs["cv_ln_b"][0]))
    put("pscale", _vec_cols(inputs["pool_scale"][0]))
    put("lnmix_g", _vec_cols(inputs["ln_mix_g"]))
    put("lnmix_b", _vec_cols(inputs["ln_mix_b"]))
    put("lnxa_g", _vec_cols(inputs["ln_xa_g"]))
    put("lnxa_b", _vec_cols(inputs["ln_xa_b"]))
    put("lnffn_g", _vec_cols(inputs["ln_ffn_g"]))
    put("lnffn_b", _vec_cols(inputs["ln_ffn_b"]))
    return cfa


_NC_CACHE = {}


def run(inputs, NG, enable=("mix0", "xa0", "ffn0", "mix1", "xa1", "ffn1"), trace=False):
    inputs = {k: np.asarray(v) for k, v in inputs.items()}
    x = inputs["x"].astype(np.float32, copy=False)
    B, S, _ = x.shape
    half_len = NG * T
    assert S == 2 * half_len
    key = (NG, tuple(enable))
    if key not in _NC_CACHE:
        _NC_CACHE[key] = build(NG, enable)
    nc = _NC_CACHE[key]
    f32 = lambda a: np.ascontiguousarray(np.asarray(a, np.float32))
    shared = {
        "w_in": f32(inputs["ab_w_in"][0]),
        "w_out": f32(inputs["ab_w_out"][0]),
        "pool_w": f32(inputs["pool_w"][0].reshape(D, 256)),
    }
    for l in range(2):
        shared[f"wq{l}"] = f32(inputs["xa_wq"][l])
        shared[f"wk{l}"] = f32(inputs["xa_wk"][l])
        shared[f"wv{l}"] = f32(inputs["xa_wv"][l])
        shared[f"wo{l}"] = f32(inputs["xa_wo"][l])
        shared[f"wg{l}"] = f32(inputs["ffn_wg"][l])
        shared[f"wu{l}"] = f32(inputs["ffn_wu"][l])
        shared[f"wd{l}"] = f32(inputs["ffn_wd"][l])
    ncores = int(os.environ.get("MK_CORES", 2 * B))
    in_maps = []
    for i in range(ncores):
        b, half = i // 2, i % 2
        m = {k: v for k, v in shared.items() if k in nc._used_inputs}
        m["xcur"] = f32(x[b, half * half_len:(half + 1) * half_len])
        m["xprev"] = f32(x[b, 0:half_len]) if half == 1 else np.zeros((half_len, D), np.float32)
        m["mem"] = f32(inputs["mem"][b])
        m["cf"] = make_cf(inputs, half, half == 0)
        in_maps.append(m)
    res = run_bass_kernel_spmd(nc, in_maps, core_ids=list(range(ncores)), **({"trace": True} if trace else {}))
    out = np.empty((B, S, D), np.float32)
    for i in range(ncores):
        b, half = i // 2, i % 2
        out[b, half * half_len:(half + 1) * half_len] = res.results[i]["out"]
    return out, res


def kernel(**inputs):
    out, _ = run(inputs, NG_FULL)
    return out
```
